# Optimizing a Trainium2 kernel written in Bass

```python
import jax, jax.numpy as jnp
from jax import lax
import numpy as np

D_MODEL = 2048
BATCH = 2
SEQ = 8192
DEPTH = 4

HEAD_DIM = 128
GRID_W = 64
BLOCK = 128
EPS = 1e-6
NEG_INF = -1e30
NA_HEADS = 8
NA_WIN_R = 8
NA_WIN_C = 16
GQ_HEADS = 8
GKV_HEADS = 2
ROPE_BASE = 10000.0
DIL_GROUPS = ((128, 1), (512, 4), (2048, 16))
DIL_HEADS_PER_GROUP = 4
DIL_HEADS = 12
ALIBI_MAX_EXP = 8.0
WA = NA_HEADS * HEAD_DIM
WB_Q = GQ_HEADS * HEAD_DIM
WB_KV = GKV_HEADS * HEAD_DIM
WC = DIL_HEADS * HEAD_DIM
WC_OUT = DIL_HEADS_PER_GROUP * HEAD_DIM
SPLIT_SIZES = (WA, WA, WA, WB_Q, WB_KV, WB_KV, WC, WC, WC, WA, WB_Q, WC_OUT, 3 * D_MODEL)
N_IN = WA * 4 + WB_Q * 2 + WB_KV * 2 + WC * 3 + WC_OUT + 3 * D_MODEL

kernel_name = "hybrid_gated_parallel_encoder"


def rmsnorm(x, g):
    x32 = x.astype(jnp.float32)
    y = x32 * lax.rsqrt(jnp.mean(x32 * x32, axis=-1, keepdims=True) + EPS)
    return (y * g.astype(jnp.float32)).astype(x.dtype)


def rope_1d(x, pos):
    half = x.shape[-1] // 2
    freqs = ROPE_BASE ** (-jnp.arange(half, dtype=jnp.float32) / half)
    ang = pos.astype(jnp.float32)[:, None] * freqs[None, :]
    cos = jnp.cos(ang)[None, :, None, :].astype(x.dtype)
    sin = jnp.sin(ang)[None, :, None, :].astype(x.dtype)
    x1, x2 = x[..., :half], x[..., half:]
    return jnp.concatenate([x1 * cos - x2 * sin, x1 * sin + x2 * cos], axis=-1)


def axial_rope(x):
    t = jnp.arange(x.shape[1])
    half = x.shape[-1] // 2
    return jnp.concatenate([rope_1d(x[..., :half], t // GRID_W),
                            rope_1d(x[..., half:], t % GRID_W)], axis=-1)


def neighbourhood_attention(q, k, v, rpb):
    bsz, seq, heads, e = q.shape
    rows = seq // GRID_W
    kr = min(NA_WIN_R, rows)
    qg = q.reshape(bsz, rows, GRID_W, heads, e)
    kg = k.reshape(bsz, rows, GRID_W, heads, e)
    vg = v.reshape(bsz, rows, GRID_W, heads, e)
    j = jnp.arange(GRID_W)
    c = jnp.arange(GRID_W)
    cs = jnp.clip(j - NA_WIN_C // 2, 0, GRID_W - NA_WIN_C)
    col_ok = (c[None, :] >= cs[:, None]) & (c[None, :] < cs[:, None] + NA_WIN_C)
    dc_idx = jnp.clip(c[None, :] - j[:, None] + NA_WIN_C - 1, 0, 2 * NA_WIN_C - 2)
    scale = e ** -0.5

    def row_block(r):
        start = jnp.clip(r - kr // 2, 0, rows - kr)
        q_r = lax.dynamic_index_in_dim(qg, r, axis=1, keepdims=False)
        k_r = lax.dynamic_slice_in_dim(kg, start, kr, axis=1)
        v_r = lax.dynamic_slice_in_dim(vg, start, kr, axis=1)
        dr_idx = start + jnp.arange(kr) - r + NA_WIN_R - 1
        bias = rpb[:, dr_idx[None, :, None], dc_idx[:, None, :]]
        s = jnp.einsum('bjhe,bkche->bhjkc', q_r, k_r,
                       preferred_element_type=jnp.float32) * scale + bias.astype(jnp.float32)[None]
        s = jnp.where(col_ok[:, None, :], s, NEG_INF)
        p = jax.nn.softmax(s.reshape(bsz, heads, GRID_W, kr * GRID_W), axis=-1).reshape(s.shape)
        return jnp.einsum('bhjkc,bkche->bjhe', p.astype(v.dtype), v_r)

    o = lax.map(row_block, jnp.arange(rows))
    return o.transpose(1, 0, 2, 3, 4).reshape(bsz, seq, heads * e)


def gqa_attention(q, k, v):
    bsz, seq, hq, e = q.shape
    hkv = k.shape[2]
    grp = hq // hkv
    nb = seq // BLOCK
    scale = e ** -0.5
    qb = q.reshape(bsz, nb, BLOCK, hkv, grp, e).transpose(1, 0, 2, 3, 4, 5)

    def block(qi):
        s = jnp.einsum('bqkge,bske->bkgqs', qi, k, preferred_element_type=jnp.float32) * scale
        p = jax.nn.softmax(s, axis=-1)
        return jnp.einsum('bkgqs,bske->bqkge', p.astype(v.dtype), v)

    o = lax.map(block, qb)
    return o.transpose(1, 0, 2, 3, 4, 5).reshape(bsz, seq, hq * e)


def dilated_group(q, k, v, dil, reach, slopes):
    bsz, seq, hg, e = q.shape
    length = seq // dil
    nb = -(-length // BLOCK)
    lp = nb * BLOCK
    kb_len = BLOCK + 2 * reach
    scale = e ** -0.5

    def to_sub(t):
        return t.reshape(bsz, length, dil, hg, e).transpose(0, 2, 3, 1, 4)

    qs = jnp.pad(to_sub(q), ((0, 0), (0, 0), (0, 0), (0, lp - length), (0, 0)))
    qs = qs.reshape(bsz, dil, hg, nb, BLOCK, e)
    pad_kv = ((0, 0), (0, 0), (0, 0), (reach, lp - length + reach), (0, 0))
    ks = jnp.pad(to_sub(k), pad_kv)
    vs = jnp.pad(to_sub(v), pad_kv)
    idx = jnp.arange(nb)[:, None] * BLOCK + jnp.arange(kb_len)[None, :]
    kbk = ks[:, :, :, idx]
    kbv = vs[:, :, :, idx]
    lq = jnp.arange(nb)[:, None] * BLOCK + jnp.arange(BLOCK)[None, :]
    lk = idx - reach
    dist = jnp.abs(lq[:, :, None] - lk[:, None, :])
    valid = (dist <= reach) & ((lk >= 0) & (lk < length))[:, None, :]
    s = jnp.einsum('bdhnqe,bdhnke->bdhnqk', qs, kbk, preferred_element_type=jnp.float32) * scale
    s = s - slopes[:, None, None, None] * (dil * dist).astype(jnp.float32)
    s = jnp.where(valid, s, NEG_INF)
    lse = jax.nn.logsumexp(s, axis=-1)
    p = jnp.exp(s - lse[..., None])
    o = jnp.einsum('bdhnqk,bdhnke->bdhnqe', p.astype(v.dtype), kbv)
    o = o.reshape(bsz, dil, hg, lp, e)[:, :, :, :length]
    o = o.transpose(0, 3, 1, 2, 4).reshape(bsz, seq, hg, e)
    lse = lse.reshape(bsz, dil, hg, lp)[..., :length].transpose(0, 3, 1, 2).reshape(bsz, seq, hg)
    return o, lse


def dilated_mixture(q, k, v):
    bsz, seq = q.shape[0], q.shape[1]
    slopes = 2.0 ** (-ALIBI_MAX_EXP * jnp.arange(1, DIL_HEADS + 1, dtype=jnp.float32) / DIL_HEADS)
    outs, lses = [], []
    for g, (win, dil) in enumerate(DIL_GROUPS):
        sl = slice(g * DIL_HEADS_PER_GROUP, (g + 1) * DIL_HEADS_PER_GROUP)
        o, l = dilated_group(q[:, :, sl], k[:, :, sl], v[:, :, sl], dil, (win // 2) // dil, slopes[sl])
        outs.append(o)
        lses.append(l)
    wts = jax.nn.softmax(jnp.stack(lses, axis=-1), axis=-1)
    o = jnp.sum(jnp.stack(outs, axis=3) * wts[..., None].astype(q.dtype), axis=3)
    return o.reshape(bsz, seq, WC_OUT)


def setup_inputs(seed: int = 0) -> dict:
    key = jax.random.key(seed)
    ks = jax.random.split(key, 12)
    f32 = jnp.float32
    return {
        "x": jax.random.normal(ks[0], (BATCH, SEQ, D_MODEL), f32),
        "pre_norm_g": 1.0 + 0.02 * jax.random.normal(ks[1], (DEPTH, D_MODEL), f32),
        "w_in": jax.random.normal(ks[2], (DEPTH, D_MODEL, N_IN), f32) * D_MODEL ** -0.5,
        "b_gate": 0.1 * jax.random.normal(ks[3], (DEPTH, 3 * D_MODEL), f32),
        "q_norm_g": 1.0 + 0.02 * jax.random.normal(ks[4], (DEPTH, HEAD_DIM), f32),
        "k_norm_g": 1.0 + 0.02 * jax.random.normal(ks[5], (DEPTH, HEAD_DIM), f32),
        "rpb": 0.1 * jax.random.normal(ks[6], (DEPTH, NA_HEADS, 2 * NA_WIN_R - 1, 2 * NA_WIN_C - 1), f32),
        "w_branch_a": jax.random.normal(ks[7], (DEPTH, WA, D_MODEL), f32) * WA ** -0.5,
        "w_branch_b": jax.random.normal(ks[8], (DEPTH, WB_Q, D_MODEL), f32) * WB_Q ** -0.5,
        "w_branch_c": jax.random.normal(ks[9], (DEPTH, WC_OUT, D_MODEL), f32) * WC_OUT ** -0.5,
        "w_out": jax.random.normal(ks[10], (DEPTH, D_MODEL, D_MODEL), f32) * D_MODEL ** -0.5,
        "post_norm_g": 1.0 + 0.02 * jax.random.normal(ks[11], (DEPTH, D_MODEL), f32),
    }


def reference(x, pre_norm_g, w_in, b_gate, q_norm_g, k_norm_g, rpb, w_branch_a, w_branch_b,
              w_branch_c, w_out, post_norm_g):
    bsz, seq = x.shape[0], x.shape[1]
    split_points = [int(p) for p in np.cumsum(SPLIT_SIZES)[:-1]]

    def heads(t, n):
        return t.reshape(bsz, seq, n, HEAD_DIM)

    for l in range(DEPTH):
        h = rmsnorm(x, pre_norm_g[l])
        proj = jnp.einsum('bsd,dn->bsn', h, w_in[l])
        (qa, ka, va, qb, kb, vb, qc, kc, vc, za, zb, zc, gates) = jnp.split(proj, split_points, axis=-1)
        ya = neighbourhood_attention(heads(qa, NA_HEADS), heads(ka, NA_HEADS), heads(va, NA_HEADS), rpb[l])
        qb_h = axial_rope(rmsnorm(heads(qb, GQ_HEADS), q_norm_g[l]))
        kb_h = axial_rope(rmsnorm(heads(kb, GKV_HEADS), k_norm_g[l]))
        yb = gqa_attention(qb_h, kb_h, heads(vb, GKV_HEADS))
        yc = dilated_mixture(heads(qc, DIL_HEADS), heads(kc, DIL_HEADS), heads(vc, DIL_HEADS))
        g = jax.nn.sigmoid((gates + b_gate[l]).astype(jnp.float32)).astype(x.dtype)
        ga, gb, gc = jnp.split(g, 3, axis=-1)
        merged = (ga * jnp.einsum('bsw,wd->bsd', ya * jax.nn.silu(za), w_branch_a[l])
                  + gb * jnp.einsum('bsw,wd->bsd', yb * jax.nn.silu(zb), w_branch_b[l])
                  + gc * jnp.einsum('bsw,wd->bsd', yc * jax.nn.silu(zc), w_branch_c[l]))
        out = jnp.einsum('bsd,de->bse', merged, w_out[l])
        x = x + rmsnorm(out, post_norm_g[l])
    return x
```

```python
import numpy as np
import ml_dtypes
from contextlib import ExitStack
import concourse.bass as bass
import concourse.mybir as mybir
from concourse.bass_utils import run_bass_kernel_spmd

F32 = mybir.dt.float32
BF16 = mybir.dt.bfloat16
AF = mybir.ActivationFunctionType
ALU = mybir.AluOpType
NPBF = ml_dtypes.bfloat16

D = 2048
T = 2048
SEQ = 8192
NIN = 17920
EPS = 1e-6
SCALE = 128 ** -0.5
NEG = -30000.0
F_QA, F_KA, F_QB, F_KB, F_QC, F_KC, F_ZA, F_ZB, F_ZC, F_G = 0, 8, 16, 24, 26, 38, 50, 58, 66, 70
NF = 118
NV = 2816
DILS = (1, 4, 16)
DEBUG_UT = False


class Prog:
    def __init__(self, nc):
        self.nc = nc
        self.E = dict(pe=nc.tensor, act=nc.scalar, dve=nc.vector, pool=nc.gpsimd, sp=nc.sync)
        self.csem = {e: nc.alloc_semaphore(name="c_" + e) for e in ("pe", "act", "dve", "pool")}
        self.ccnt = {e: 0 for e in self.csem}
        self.dsem = {}
        self.res = {}
        self.seen = {}
        self.nsem = 0

    def _wait(self, e, toks):
        best = {}
        for t in toks:
            if t is None:
                continue
            s, v = t
            k = id(s)
            if k not in best or best[k][1] < v:
                best[k] = (s, v)
        for k, (s, v) in best.items():
            if e == "pe" and s is self.csem["pe"]:
                continue
            if self.seen.get((e, k), 0) >= v:
                continue
            self.seen[(e, k)] = v
            self.E[e].wait_ge(s, v)

    def _deps(self, r, w):
        toks = []
        for x in r:
            st = self.res.setdefault(x, [None, {}])
            toks.append(st[0])
        for x in w:
            st = self.res.setdefault(x, [None, {}])
            toks.append(st[0])
            toks.extend(st[1].values())
        return toks

    def _commit(self, tok, r, w):
        for x in r:
            rd = self.res[x][1]
            k = id(tok[0])
            if k not in rd or rd[k][1] < tok[1]:
                rd[k] = tok
        for x in w:
            self.res[x] = [tok, {}]

    def op(self, e, fn, r=(), w=()):
        self._wait(e, self._deps(r, w))
        ins = fn(self.E[e])
        self.ccnt[e] += 1
        ins.then_inc(self.csem[e], 1)
        self._commit((self.csem[e], self.ccnt[e]), r, w)

    def dma(self, q, out, in_, key, r=(), w=()):
        if key not in self.dsem:
            self.nsem += 1
            self.dsem[key] = [self.nc.alloc_semaphore(name="d%d" % self.nsem), 0]
        d = self.dsem[key]
        toks = [t for t in self._deps(r, w) if t is not None and t[0] is not d[0]]
        self._wait(q, toks)
        d[1] += 16
        self.E[q].dma_start(out=out, in_=in_).then_inc(d[0], 16)
        self._commit((d[0], d[1]), r, w)

    def barrier(self):
        toks = [(s, c) for s, c in ((self.csem[e], self.ccnt[e]) for e in self.csem) if c > 0]
        toks += [(s, v) for (s, v) in self.dsem.values()]
        for e in self.E:
            self._wait(e, toks)
        self.res = {}

    def finish(self):
        self._wait("sp", [(s, v) for (s, v) in self.dsem.values()])


def _mm_group(P, ps_ap, pairs, r, w, first=True, last=True):
    def fn(pe):
        ins = None
        n = len(pairs)
        for i, (l, rh) in enumerate(pairs):
            ins = pe.matmul(ps_ap, l, rh, start=(first and i == 0), stop=(last and i == n - 1))
        return ins
    P.op("pe", fn, r=r, w=w)


def _chunk_kind(c):
    col = c * 128
    if col < 1024: return ("copy", F_QA + c)
    if col < 2048: return ("copy", F_KA + (c - 8))
    if col < 3072: return ("v", None)
    if col < 4096: return ("ropeq", F_QB + (c - 24))
    if col < 4352: return ("ropek", F_KB + (c - 32))
    if col < 4608: return ("v", None)
    if col < 6144: return ("copy", F_QC + (c - 36))
    if col < 7680: return ("copy", F_KC + (c - 48))
    if col < 9216: return ("v", None)
    if col < 10240: return ("silu", F_ZA + (c - 72))
    if col < 11264: return ("silu", F_ZB + (c - 80))
    if col < 11776: return ("silu", F_ZC + (c - 88))
    return ("sig", F_G + (c - 92))


def _vcol(col):
    if col < 3072: return col - 2048
    if col < 4608: return 1024 + col - 4352
    return 1280 + col - 7680


def build_p1():
    nc = bass.Bass("TRN2", target_bir_lowering=False)
    xT = nc.dram_tensor("xT", [D, T], F32, kind="ExternalInput").ap()
    w_in = nc.dram_tensor("w_in", [D, NIN], F32, kind="ExternalInput").ap()
    cst = nc.dram_tensor("cst", [128, 68], F32, kind="ExternalInput").ap()
    cossin = nc.dram_tensor("cossin", [128, 2, T], F32, kind="ExternalInput").ap()
    rm = nc.dram_tensor("rm", [128, 128], F32, kind="ExternalInput").ap()
    FT = nc.dram_tensor("FT", [NF, 128, T], BF16, kind="ExternalOutput").ap()
    VT = nc.dram_tensor("VT", [T, NV], BF16, kind="ExternalOutput").ap()
    P = Prog(nc)
    with ExitStack() as es:
        def sb(name, shape, dt):
            return es.enter_context(nc.sbuf_tensor(name, shape, dt))
        hT = sb("hT", [128, 16, T], BF16)
        xin = sb("xin", [128, 16, 256], F32)
        wbl = [sb("wbl%d" % i, [128, 16, 256], BF16) for i in range(3)]
        stg = [sb("stg%d" % i, [128, T], BF16) for i in range(2)]
        vst = [sb("vst%d" % i, [128, 16, 256], BF16) for i in range(2)]
        cs_sb = sb("cs_sb", [128, 2, T], F32)
        cst_sb = sb("cst_sb", [128, 68], F32)
        rm_sb = sb("rm_sb", [128, 128], BF16)
        ones = sb("ones", [128, 128], BF16)
        sq = [sb("sq%d" % i, [128, 512], BF16) for i in range(2)]
        qb16 = sb("qb16", [128, 512], BF16)
        sd = sb("sd", [128, 512], F32)
        rstd = sb("rstd", [128, 512], F32)
        t1 = sb("t1", [128, 512], F32)
        t2 = sb("t2", [128, 512], F32)
        t3 = sb("t3", [128, 512], F32)
        ps = es.enter_context(nc.psum_tensor("ps", [128, 8, 512], F32))

        P.dma("sp", cst_sb[:], cst, "cst", w=["cst"])
        P.dma("sp", cs_sb[:], cossin, "cs", w=["cs"])
        P.dma("pool", rm_sb[:], rm, "rm", w=["rm"])
        P.op("pool", lambda g: g.memset(ones[:], 1.0), w=["ones"])

        xv = xT.rearrange("(k p) t -> p k t", p=128)
        for t8 in range(8):
            ts = slice(t8 * 256, (t8 + 1) * 256)
            P.dma("sp", xin[:], xv[:, :, ts], "xin", w=["xin"])
            for kc in range(16):
                s = kc % 2
                P.op("act", lambda a, kc=kc, s=s: a.activation(out=sq[s][:, 0:256], in_=xin[:, kc, :], func=AF.Square),
                     r=["xin"], w=["sq%d" % s])
                P.op("pe", lambda pe, kc=kc, s=s: pe.matmul(ps[:, 7, 0:256], ones[:], sq[s][:, 0:256],
                                                           start=(kc == 0), stop=(kc == 15)),
                     r=["sq%d" % s, "ones"], w=["ps7"])
            P.op("act", lambda a: a.activation(out=sd[:, 0:256], in_=ps[:, 7, 0:256], func=AF.Sqrt,
                                               bias=EPS, scale=1.0 / D), r=["ps7"], w=["sd"])
            P.op("dve", lambda v: v.reciprocal(rstd[:, 0:256], sd[:, 0:256]), r=["sd"], w=["rstd"])
            for kc in range(16):
                P.op("dve", lambda v, kc=kc, ts=ts: v.scalar_tensor_tensor(
                    out=hT[:, kc, ts], in0=xin[:, kc, :], scalar=cst_sb[:, kc:kc + 1], in1=rstd[:, 0:256],
                    op0=ALU.mult, op1=ALU.mult), r=["xin", "rstd", "cst"], w=["hT"])

        wv = w_in.rearrange("(k p) n -> p k n", p=128)
        NB = NIN // 256

        def load_w(b):
            P.dma("pool", wbl[b % 3][:], wv[:, :, b * 256:(b + 1) * 256], "w%d" % (b % 3), w=["w%d" % (b % 3)])

        load_w(0)
        load_w(1)
        bank = [0]
        cp = [0]
        vti = [0]

        def nbank():
            b = bank[0]
            bank[0] = (b + 1) % 6
            return b

        def evac_copy(out_ap, in_ap, r, w):
            cp[0] ^= 1
            if cp[0]:
                P.op("act", lambda a: a.activation(out=out_ap, in_=in_ap, func=AF.Copy), r=r, w=w)
            else:
                P.op("dve", lambda v: v.tensor_copy(out_ap, in_ap), r=r, w=w)

        for b in range(NB):
            if b + 2 < NB:
                load_w(b + 2)
            ws = b % 3
            wt = wbl[ws]
            kind0 = _chunk_kind(2 * b)[0]
            if kind0 == "v":
                vs = vti[0] % 2
                vti[0] += 1
                for s16 in range(16):
                    bk = nbank()
                    _mm_group(P, ps[:, bk, 0:256],
                              [(hT[:, kc, s16 * 128:(s16 + 1) * 128], wt[:, kc, :]) for kc in range(16)],
                              r=["w%d" % ws, "hT"], w=["ps%d" % bk])
                    evac_copy(vst[vs][:, s16, :], ps[:, bk, 0:256], ["ps%d" % bk], ["vst%d" % vs])
                vc0 = _vcol(b * 256)
                P.dma("sp", VT.rearrange("(s p) c -> p s c", p=128)[:, :, vc0:vc0 + 256], vst[vs][:],
                      "vst%d" % vs, r=["vst%d" % vs])
                continue
            for ci in range(2):
                c = 2 * b + ci
                kind, fi = _chunk_kind(c)
                ss = c % 2
                for tt in range(4):
                    tsl = slice(tt * 512, (tt + 1) * 512)
                    bk = nbank()
                    _mm_group(P, ps[:, bk, :],
                              [(wt[:, kc, ci * 128:(ci + 1) * 128], hT[:, kc, tsl]) for kc in range(16)],
                              r=["w%d" % ws, "hT"], w=["ps%d" % bk])
                    pr = ["ps%d" % bk]
                    so = stg[ss][:, tsl]
                    sw = ["stg%d" % ss]
                    if kind == "copy":
                        evac_copy(so, ps[:, bk, :], pr, sw)
                    elif kind == "silu":
                        P.op("act", lambda a, so=so, bk=bk: a.activation(out=so, in_=ps[:, bk, :], func=AF.Silu),
                             r=pr, w=sw)
                    elif kind == "sig":
                        gi = 16 + (fi - F_G)
                        P.op("act", lambda a, so=so, bk=bk, gi=gi: a.activation(
                            out=so, in_=ps[:, bk, :], func=AF.Sigmoid, bias=cst_sb[:, gi:gi + 1]),
                            r=pr + ["cst"], w=sw)
                    else:
                        gcol = 64 if kind == "ropeq" else 66
                        P.op("act", lambda a, bk=bk: a.activation(out=qb16[:], in_=ps[:, bk, :], func=AF.Copy),
                             r=pr, w=["qb16"])
                        P.op("act", lambda a, bk=bk: a.activation(out=sq[0][:], in_=ps[:, bk, :], func=AF.Square),
                             r=pr, w=["sq0"])
                        P.op("pe", lambda pe: pe.matmul(ps[:, 6, :], ones[:], sq[0][:], start=True, stop=True),
                             r=["sq0", "ones"], w=["ps6"])
                        P.op("pe", lambda pe: pe.matmul(ps[:, 7, :], rm_sb[:], qb16[:], start=True, stop=True),
                             r=["qb16", "rm"], w=["ps7"])
                        P.op("act", lambda a: a.activation(out=sd[:], in_=ps[:, 6, :], func=AF.Sqrt,
                                                           bias=EPS, scale=1.0 / 128), r=["ps6"], w=["sd"])
                        P.op("dve", lambda v: v.reciprocal(rstd[:], sd[:]), r=["sd"], w=["rstd"])
                        P.op("dve", lambda v, bk=bk, tsl=tsl, gcol=gcol: v.scalar_tensor_tensor(
                            out=t1[:], in0=ps[:, bk, :], scalar=cst_sb[:, gcol:gcol + 1], in1=cs_sb[:, 0, tsl],
                            op0=ALU.mult, op1=ALU.mult), r=pr + ["cst", "cs"], w=["t1"])
                        P.op("dve", lambda v, tsl=tsl, gcol=gcol: v.scalar_tensor_tensor(
                            out=t2[:], in0=ps[:, 7, :], scalar=cst_sb[:, gcol + 1:gcol + 2], in1=cs_sb[:, 1, tsl],
                            op0=ALU.mult, op1=ALU.mult), r=["ps7", "cst", "cs"], w=["t2"])
                        P.op("pool", lambda g: g.tensor_tensor(t3[:], t1[:], t2[:], ALU.add),
                             r=["t1", "t2"], w=["t3"])
                        P.op("dve", lambda v, so=so: v.tensor_tensor(so, t3[:], rstd[:], ALU.mult),
                             r=["t3", "rstd"], w=sw)
                P.dma("sp", FT[fi], stg[ss][:], "stg%d" % ss, r=["stg%d" % ss])
        P.finish()
    return nc


class Arena:
    def __init__(self, t, n):
        self.t, self.n, self.o = t, n, 0

    def reset(self):
        self.o = 0

    def alloc(self, *shape):
        sz = int(np.prod(shape))
        sz_al = (sz + 31) // 32 * 32
        assert self.o + sz_al <= self.n, (self.o, sz_al, self.n)
        ap = self.t[:, self.o:self.o + sz]
        self.o += sz_al
        if len(shape) == 2:
            return ap.rearrange("p (a b) -> p a b", b=shape[1])
        return ap


def build_p2():
    nc = bass.Bass("TRN2", target_bir_lowering=False)
    dt = lambda name, shape, d, kind="ExternalInput": nc.dram_tensor(name, shape, d, kind=kind).ap()
    FT = dt("FT", [NF, 128, T], BF16)
    KB = dt("KB", [2, 128, SEQ], BF16)
    VB = dt("VB", [SEQ, 256], BF16)
    KA = dt("KA", [8, 128, 2560], BF16)
    VA = dt("VA", [2560, 1024], BF16)
    KC = [dt("KC%d" % g, [4, 128, T + 128 * d], BF16) for g, d in enumerate(DILS)]
    VC = [dt("VC%d" % g, [T + 128 * d, 512], BF16) for g, d in enumerate(DILS)]
    T3 = dt("T3", [8, 64, 23, 64], F32)
    VALID = dt("VALID", [32, 128, 512], BF16)
    CB = dt("CB", [128, 2], F32)
    BC = dt("BC", [12, 128, 256], F32)
    xT = dt("xT", [D, T], F32)
    wa = dt("wa", [1024, D], F32)
    wb = dt("wb", [1024, D], F32)
    wc = dt("wc", [512, D], F32)
    wo = dt("wo", [D, D], F32)
    pg = dt("pg", [128, 16], F32)
    XO = dt("XO", [D, T], F32, kind="ExternalOutput")
    UT = nc.dram_tensor("UT", [20, 128, T], BF16, kind=("ExternalOutput" if DEBUG_UT else "Internal")).ap()
    P = Prog(nc)
    with ExitStack() as es:
        NBF, NFP = 52 * 1024, 13 * 1024
        abf_t = es.enter_context(nc.sbuf_tensor("abf", [128, NBF], BF16))
        afp_t = es.enter_context(nc.sbuf_tensor("afp", [128, NFP], F32))
        ones = es.enter_context(nc.sbuf_tensor("ones", [128, 128], BF16))
        zeros = es.enter_context(nc.sbuf_tensor("zeros", [128, 128], BF16))
        cb_sb = es.enter_context(nc.sbuf_tensor("cb_sb", [128, 2], F32))
        pg_sb = es.enter_context(nc.sbuf_tensor("pg_sb", [128, 16], F32))
        ps = es.enter_context(nc.psum_tensor("ps", [128, 8, 512], F32))
        BFA = Arena(abf_t, NBF)
        FPA = Arena(afp_t, NFP)
        P.op("pool", lambda g: g.memset(ones[:], 1.0), w=["ones"])
        P.op("pool", lambda g: g.memset(zeros[:], 0.0), w=["zeros"])
        P.dma("sp", cb_sb[:], CB, "cb", w=["cb"])
        P.dma("sp", pg_sb[:], pg, "pg", w=["pg"])

        def finalize(qs, nb, db, rden, y, zt, ub, dst):
            P.op("dve", lambda v: v.reciprocal(rden[qs], ps[:, db, :]), r=["ps%d" % db], w=["rden%d" % qs])
            P.op("dve", lambda v: v.tensor_tensor(y[qs], ps[:, nb, :], rden[qs], ALU.mult),
                 r=["ps%d" % nb, "rden%d" % qs], w=["y%d" % qs])
            P.op("pool", lambda g: g.tensor_tensor(ub[qs], y[qs], zt[qs], ALU.mult),
                 r=["y%d" % qs, "z%d" % qs], w=["ub%d" % qs])
            P.dma("sp", dst, ub[qs], "ub%d" % qs, r=["ub%d" % qs])

        kbT = BFA.alloc(2, SEQ)
        vb = BFA.alloc(64, 256)
        qT = [BFA.alloc(512) for _ in range(2)]
        zt = [BFA.alloc(512) for _ in range(2)]
        ub = [BFA.alloc(512) for _ in range(2)]
        pT = [BFA.alloc(512) for _ in range(3)]
        rden = [FPA.alloc(512) for _ in range(2)]
        y = [FPA.alloc(512) for _ in range(2)]
        for kv in range(2):
            P.dma("sp", kbT[:, kv, :], KB[kv], "kbT", w=["kbT"])
        P.dma("sp", vb, VB.rearrange("(k p) c -> p k c", p=128), "vb", w=["vb"])
        it = 0
        for h in range(8):
            kv = h // 4
            for qt in range(4):
                qs = it % 2
                it += 1
                tsl = slice(qt * 512, (qt + 1) * 512)
                P.dma("sp", qT[qs], FT[F_QB + h, :, tsl], "q%d" % qs, w=["q%d" % qs])
                P.dma("sp", zt[qs], FT[F_ZB + h, :, tsl], "z%d" % qs, w=["z%d" % qs])
                nb, db = 3 + 2 * qs, 4 + 2 * qs

                def qk(kt, qs=qs, kv=kv):
                    s = kt % 3
                    P.op("pe", lambda pe: pe.matmul(ps[:, s, :], kbT[:, kv, kt * 128:(kt + 1) * 128], qT[qs],
                                                    start=True, stop=True),
                         r=["kbT", "q%d" % qs], w=["ps%d" % s])
                qk(0)
                qk(1)
                for kt in range(64):
                    s = kt % 3
                    if kt + 2 < 64:
                        qk(kt + 2)
                    P.op("act", lambda a, s=s: a.activation(out=pT[s], in_=ps[:, s, :], func=AF.Exp, scale=SCALE),
                         r=["ps%d" % s], w=["p%d" % s])
                    P.op("pe", lambda pe, s=s, kt=kt, kv=kv, nb=nb: pe.matmul(
                        ps[:, nb, :], vb[:, kt, kv * 128:(kv + 1) * 128], pT[s], start=(kt == 0), stop=(kt == 63)),
                        r=["vb", "p%d" % s], w=["ps%d" % nb])
                    P.op("pe", lambda pe, s=s, kt=kt, db=db: pe.matmul(
                        ps[:, db, :], ones[:], pT[s], start=(kt == 0), stop=(kt == 63)),
                        r=["ones", "p%d" % s], w=["ps%d" % db])
                finalize(qs, nb, db, rden, y, zt, ub, UT[8 + h, :, tsl])
        P.barrier()

        BFA.reset()
        FPA.reset()
        kaT = BFA.alloc(2560)
        va = BFA.alloc(20, 128)
        ebias = BFA.alloc(8, 512)
        valid = BFA.alloc(32, 512)
        msk = [BFA.alloc(512) for _ in range(3)]
        p2 = [BFA.alloc(512) for _ in range(3)]
        qT = [BFA.alloc(512) for _ in range(2)]
        zt = [BFA.alloc(512) for _ in range(2)]
        ub = [BFA.alloc(512) for _ in range(2)]
        pT = [BFA.alloc(512) for _ in range(3)]
        rden = [FPA.alloc(512) for _ in range(2)]
        y = [FPA.alloc(512) for _ in range(2)]
        t3s = [FPA.alloc(512) for _ in range(2)]
        P.dma("sp", valid, VALID.rearrange("n p q -> p n q"), "valid", w=["valid"])
        it = 0
        e_i = 0
        for h in range(8):
            P.dma("sp", kaT, KA[h], "kaT", w=["kaT"])
            P.dma("sp", va, VA.rearrange("(k p) c -> p k c", p=128)[:, :, h * 128:(h + 1) * 128], "va", w=["va"])
            for kt in range(8):
                es_ = e_i % 2
                e_i += 1
                for i in range(2):
                    m0 = 15 - 2 * kt - i
                    P.dma("sp", t3s[es_][i * 64:(i + 1) * 64, :],
                          T3[h, :, m0:m0 + 8, :].rearrange("k m q -> k (m q)"), "t3s%d" % es_, w=["t3s%d" % es_])
                P.op("act", lambda a, kt=kt, es_=es_: a.activation(out=ebias[:, kt, :], in_=t3s[es_], func=AF.Exp),
                     r=["t3s%d" % es_], w=["ebias"])
            for g in range(4):
                qs = it % 2
                it += 1
                tsl = slice(g * 512, (g + 1) * 512)
                P.dma("sp", qT[qs], FT[F_QA + h, :, tsl], "q%d" % qs, w=["q%d" % qs])
                P.dma("sp", zt[qs], FT[F_ZA + h, :, tsl], "z%d" % qs, w=["z%d" % qs])
                nb, db = 3 + 2 * qs, 4 + 2 * qs

                def qk(kt, qs=qs, g=g):
                    s = kt % 3
                    ktile = 4 * g + kt
                    P.op("pe", lambda pe: pe.matmul(ps[:, s, :], kaT[:, ktile * 128:(ktile + 1) * 128], qT[qs],
                                                    start=True, stop=True),
                         r=["kaT", "q%d" % qs], w=["ps%d" % s])
                qk(0)
                qk(1)
                for kt in range(8):
                    s = kt % 3
                    ktile = 4 * g + kt
                    if kt + 2 < 8:
                        qk(kt + 2)
                    P.op("pool", lambda gp, s=s, kt=kt, g=g: gp.tensor_tensor(
                        msk[s], ebias[:, kt, :], valid[:, g * 8 + kt, :], ALU.mult),
                        r=["ebias", "valid"], w=["msk%d" % s])
                    P.op("act", lambda a, s=s: a.activation(out=pT[s], in_=ps[:, s, :], func=AF.Exp, scale=SCALE),
                         r=["ps%d" % s], w=["p%d" % s])
                    P.op("dve", lambda v, s=s: v.tensor_tensor(p2[s], pT[s], msk[s], ALU.mult),
                         r=["p%d" % s, "msk%d" % s], w=["pp%d" % s])
                    P.op("pe", lambda pe, s=s, kt=kt, ktile=ktile, nb=nb: pe.matmul(
                        ps[:, nb, :], va[:, ktile, :], p2[s], start=(kt == 0), stop=(kt == 7)),
                        r=["va", "pp%d" % s], w=["ps%d" % nb])
                    P.op("pe", lambda pe, s=s, kt=kt, db=db: pe.matmul(
                        ps[:, db, :], ones[:], p2[s], start=(kt == 0), stop=(kt == 7)),
                        r=["ones", "pp%d" % s], w=["ps%d" % db])
                finalize(qs, nb, db, rden, y, zt, ub, UT[h, :, tsl])
        P.barrier()

        BFA.reset()
        FPA.reset()
        kcT = [BFA.alloc(T + 128 * 16) for _ in range(2)]
        qcT = [BFA.alloc(T) for _ in range(2)]
        vct = [BFA.alloc(17, 128) for _ in range(2)]
        mC = [BFA.alloc(256) for _ in range(2)]
        pT = [BFA.alloc(512) for _ in range(3)]
        p2 = [BFA.alloc(512) for _ in range(3)]
        ztc = BFA.alloc(T)
        ubc = BFA.alloc(T)
        accn = FPA.alloc(T)
        accd = FPA.alloc(T)
        mst = [FPA.alloc(256) for _ in range(2)]
        ytmp = FPA.alloc(512)
        hi_ = 0
        vi_ = 0
        ui_ = 0
        si_ = 0
        for hh in range(4):
            for gi, d in enumerate(DILS):
                head = 4 * gi + hh
                ks = hi_ % 2
                hi_ += 1
                W = T + 128 * d
                Lq = T // d
                nkt = Lq // 128 + 1
                P.dma("sp", kcT[ks][:, 0:W], KC[gi][hh], "kc%d" % ks, w=["kc%d" % ks])
                P.dma("sp", qcT[ks], FT[F_QC + head], "qc%d" % ks, w=["qc%d" % ks])
                P.dma("sp", mst[ks], BC[head], "mst%d" % ks, w=["mst%d" % ks])
                P.op("act", lambda a, ks=ks: a.activation(out=mC[ks], in_=mst[ks], func=AF.Exp),
                     r=["mst%d" % ks], w=["mC%d" % ks])
                vsrc = VC[gi].rearrange("(k p dd) c -> dd p k c", p=128, dd=d)
                for c in range(d):
                    vs = vi_ % 2
                    vi_ += 1
                    nload = min(nkt, 17)
                    P.dma("sp", vct[vs][:, 0:nload, :], vsrc[c, :, :, hh * 128:(hh + 1) * 128], "vct%d" % vs,
                          w=["vct%d" % vs])
                    for qq in range(max(1, Lq // 512)):
                        qlo, qhi = qq * 512, min(Lq, qq * 512 + 512)
                        NQ = qhi - qlo
                        us = ui_ % 2
                        ui_ += 1
                        nb, db = 3 + 2 * us, 4 + 2 * us
                        kts = [kt for kt in range(nkt) if 128 * kt + 128 > qlo and 128 * kt - 128 < qhi]
                        P.op("pe", lambda pe, nb=nb, NQ=NQ, ks=ks: pe.matmul(
                            ps[:, nb, 0:NQ], zeros[:], qcT[ks][:, 0:NQ], start=True, stop=False,
                            skip_group_check=True), r=["zeros", "qc%d" % ks], w=["ps%d" % nb])
                        P.op("pe", lambda pe, db=db, NQ=NQ, ks=ks: pe.matmul(
                            ps[:, db, 0:NQ], zeros[:], qcT[ks][:, 0:NQ], start=True, stop=False,
                            skip_group_check=True), r=["zeros", "qc%d" % ks], w=["ps%d" % db])
                        for j, kt in enumerate(kts):
                            s = si_ % 3
                            si_ += 1
                            lo = max(qlo, 128 * kt - 128)
                            hi = min(qhi, 128 * kt + 128)
                            N = hi - lo
                            f0 = lo - (128 * kt - 128)
                            k0 = 128 * kt * d + c
                            kap = kcT[ks][:, k0:k0 + 127 * d + 1:d] if d > 1 else kcT[ks][:, k0:k0 + 128]
                            q0 = lo * d + c
                            qap = qcT[ks][:, q0:q0 + (N - 1) * d + 1:d] if d > 1 else qcT[ks][:, q0:q0 + N]
                            P.op("pe", lambda pe, s=s, kap=kap, qap=qap, N=N: pe.matmul(
                                ps[:, s, 0:N], kap, qap, start=True, stop=True),
                                r=["kc%d" % ks, "qc%d" % ks], w=["ps%d" % s])
                            if kt == 0:
                                P.op("act", lambda a, s=s, N=N: a.activation(
                                    out=pT[s][:, 0:N], in_=ps[:, s, 0:N], func=AF.Exp, scale=SCALE,
                                    bias=cb_sb[:, 0:1]), r=["ps%d" % s, "cb"], w=["p%d" % s])
                            elif kt == nkt - 1:
                                P.op("act", lambda a, s=s, N=N: a.activation(
                                    out=pT[s][:, 0:N], in_=ps[:, s, 0:N], func=AF.Exp, scale=SCALE,
                                    bias=cb_sb[:, 1:2]), r=["ps%d" % s, "cb"], w=["p%d" % s])
                            else:
                                P.op("act", lambda a, s=s, N=N: a.activation(
                                    out=pT[s][:, 0:N], in_=ps[:, s, 0:N], func=AF.Exp, scale=SCALE),
                                    r=["ps%d" % s], w=["p%d" % s])
                            P.op("dve", lambda v, s=s, N=N, f0=f0, ks=ks: v.tensor_tensor(
                                p2[s][:, 0:N], pT[s][:, 0:N], mC[ks][:, f0:f0 + N], ALU.mult),
                                r=["p%d" % s, "mC%d" % ks], w=["pp%d" % s])
                            last = (j == len(kts) - 1)
                            o0 = lo - qlo
                            P.op("pe", lambda pe, s=s, N=N, kt=kt, vs=vs, nb=nb, o0=o0, last=last: pe.matmul(
                                ps[:, nb, o0:o0 + N], vct[vs][:, kt, :], p2[s][:, 0:N], start=False, stop=last,
                                skip_group_check=True), r=["vct%d" % vs, "pp%d" % s], w=["ps%d" % nb])
                            P.op("pe", lambda pe, s=s, N=N, db=db, o0=o0, last=last: pe.matmul(
                                ps[:, db, o0:o0 + N], ones[:], p2[s][:, 0:N], start=False, stop=last,
                                skip_group_check=True), r=["ones", "pp%d" % s], w=["ps%d" % db])
                        t0 = qlo * d + c
                        an = accn[:, t0:t0 + (NQ - 1) * d + 1:d] if d > 1 else accn[:, t0:t0 + NQ]
                        ad = accd[:, t0:t0 + (NQ - 1) * d + 1:d] if d > 1 else accd[:, t0:t0 + NQ]
                        if gi == 0:
                            P.op("dve", lambda v, an=an, nb=nb, NQ=NQ: v.tensor_copy(an, ps[:, nb, 0:NQ]),
                                 r=["ps%d" % nb], w=["accn"])
                            P.op("dve", lambda v, ad=ad, db=db, NQ=NQ: v.tensor_copy(ad, ps[:, db, 0:NQ]),
                                 r=["ps%d" % db], w=["accd"])
                        else:
                            P.op("dve", lambda v, an=an, nb=nb, NQ=NQ: v.tensor_tensor(an, ps[:, nb, 0:NQ], an, ALU.add),
                                 r=["ps%d" % nb, "accn"], w=["accn"])
                            P.op("dve", lambda v, ad=ad, db=db, NQ=NQ: v.tensor_tensor(ad, ps[:, db, 0:NQ], ad, ALU.add),
                                 r=["ps%d" % db, "accd"], w=["accd"])
            P.dma("sp", ztc, FT[F_ZC + hh], "ztc", w=["ztc"])
            for qq in range(4):
                tsl = slice(qq * 512, (qq + 1) * 512)
                P.op("dve", lambda v, tsl=tsl: v.reciprocal(ytmp, accd[:, tsl]), r=["accd"], w=["ytmp"])
                P.op("dve", lambda v, tsl=tsl: v.tensor_tensor(ytmp, accn[:, tsl], ytmp, ALU.mult),
                     r=["accn", "ytmp"], w=["ytmp"])
                P.op("pool", lambda gp, tsl=tsl: gp.tensor_tensor(ubc[:, tsl], ytmp, ztc[:, tsl], ALU.mult),
                     r=["ytmp", "ztc"], w=["ubc"])
            P.dma("sp", UT[16 + hh], ubc, "ubc", r=["ubc"])
        P.barrier()

        BFA.reset()
        FPA.reset()
        ut = [BFA.alloc(20, 512) for _ in range(2)]
        g3 = [BFA.alloc(3, 512) for _ in range(2)]
        wbr = [BFA.alloc(20, 256) for _ in range(2)]
        wob = [BFA.alloc(16, 256) for _ in range(2)]
        mT = BFA.alloc(16, 512)
        sqb = [BFA.alloc(512) for _ in range(2)]
        outT = FPA.alloc(16, 512)
        m1 = [FPA.alloc(512) for _ in range(2)]
        m2 = [FPA.alloc(512) for _ in range(2)]
        xt = [FPA.alloc(512) for _ in range(2)]
        sd = FPA.alloc(512)
        rstd = FPA.alloc(512)
        tq = [FPA.alloc(512) for _ in range(2)]
        wav = wa.rearrange("(k p) n -> p k n", p=128)
        wbv = wb.rearrange("(k p) n -> p k n", p=128)
        wcv = wc.rearrange("(k p) n -> p k n", p=128)
        wov = wo.rearrange("(k p) n -> p k n", p=128)
        gv = FT[F_G:F_G + 48].rearrange("(b c) p t -> c p b t", b=3)
        utv = UT.rearrange("c p t -> p c t")
        gi_ = 0
        xi_ = 0
        bi_ = 0
        oi_ = 0
        bk_ = 0
        for tt in range(4):
            tsl = slice(tt * 512, (tt + 1) * 512)
            us = tt % 2
            P.dma("sp", ut[us], utv[:, :, tsl], "ut%d" % us, w=["ut%d" % us])
            for cb in range(8):
                ws = bi_ % 2
                bi_ += 1
                csl = slice(cb * 256, (cb + 1) * 256)
                P.dma("pool", wbr[ws][:, 0:8, :], wav[:, :, csl], "wbr%d" % ws, w=["wbr%d" % ws])
                P.dma("pool", wbr[ws][:, 8:16, :], wbv[:, :, csl], "wbr%d" % ws, w=["wbr%d" % ws])
                P.dma("pool", wbr[ws][:, 16:20, :], wcv[:, :, csl], "wbr%d" % ws, w=["wbr%d" % ws])
                for ci in range(2):
                    dc = 2 * cb + ci
                    gs = gi_ % 2
                    gi_ += 1
                    P.dma("sp", g3[gs], gv[dc][:, :, tsl], "g3%d" % gs, w=["g3%d" % gs])
                    col = slice(ci * 128, (ci + 1) * 128)
                    banks = [(bk_ + i) % 6 for i in range(3)]
                    bk_ = (bk_ + 3) % 6
                    for bi, (k0, k1) in enumerate(((0, 8), (8, 16), (16, 20))):
                        _mm_group(P, ps[:, banks[bi], :],
                                  [(wbr[ws][:, k, col], ut[us][:, k, :]) for k in range(k0, k1)],
                                  r=["wbr%d" % ws, "ut%d" % us], w=["ps%d" % banks[bi]])
                    P.op("dve", lambda v, gs=gs, b0=banks[0]: v.tensor_tensor(m1[gs], ps[:, b0, :], g3[gs][:, 0, :], ALU.mult),
                         r=["ps%d" % banks[0], "g3%d" % gs], w=["m1%d" % gs])
                    P.op("dve", lambda v, gs=gs, b1=banks[1]: v.tensor_tensor(m2[gs], ps[:, b1, :], g3[gs][:, 1, :], ALU.mult),
                         r=["ps%d" % banks[1], "g3%d" % gs], w=["m2%d" % gs])
                    P.op("pool", lambda gp, gs=gs: gp.tensor_tensor(m1[gs], m1[gs], m2[gs], ALU.add),
                         r=["m1%d" % gs, "m2%d" % gs], w=["m1%d" % gs])
                    P.op("dve", lambda v, gs=gs, b2=banks[2]: v.tensor_tensor(m2[gs], ps[:, b2, :], g3[gs][:, 2, :], ALU.mult),
                         r=["ps%d" % banks[2], "g3%d" % gs, "m1%d" % gs], w=["m2%d" % gs])
                    P.op("pool", lambda gp, gs=gs, dc=dc: gp.tensor_tensor(mT[:, dc, :], m1[gs], m2[gs], ALU.add),
                         r=["m1%d" % gs, "m2%d" % gs], w=["mT"])
            for eb in range(8):
                ws = oi_ % 2
                oi_ += 1
                csl = slice(eb * 256, (eb + 1) * 256)
                P.dma("pool", wob[ws], wov[:, :, csl], "wob%d" % ws, w=["wob%d" % ws])
                for ci in range(2):
                    ec = 2 * eb + ci
                    col = slice(ci * 128, (ci + 1) * 128)
                    bk = bk_
                    bk_ = (bk_ + 1) % 6
                    _mm_group(P, ps[:, bk, :], [(wob[ws][:, k, col], mT[:, k, :]) for k in range(16)],
                              r=["wob%d" % ws, "mT"], w=["ps%d" % bk])
                    P.op("act", lambda a, ec=ec, bk=bk: a.activation(out=outT[:, ec, :], in_=ps[:, bk, :], func=AF.Copy),
                         r=["ps%d" % bk], w=["outT"])
                    s = ec % 2
                    P.op("act", lambda a, s=s, bk=bk: a.activation(out=sqb[s], in_=ps[:, bk, :], func=AF.Square),
                         r=["ps%d" % bk], w=["sqb%d" % s])
                    P.op("pe", lambda pe, s=s, ec=ec: pe.matmul(ps[:, 7, :], ones[:], sqb[s], start=(ec == 0), stop=(ec == 15)),
                         r=["sqb%d" % s, "ones"], w=["ps7"])
            P.op("act", lambda a: a.activation(out=sd, in_=ps[:, 7, :], func=AF.Sqrt, bias=EPS, scale=1.0 / D),
                 r=["ps7"], w=["sd"])
            P.op("dve", lambda v: v.reciprocal(rstd, sd), r=["sd"], w=["rstd"])
            for ec in range(16):
                xs = xi_ % 2
                xi_ += 1
                P.dma("sp", xt[xs], xT[ec * 128:(ec + 1) * 128, tsl], "xt%d" % xs, w=["xt%d" % xs])
                P.op("dve", lambda v, ec=ec, xs=xs: v.scalar_tensor_tensor(
                    out=tq[xs], in0=outT[:, ec, :], scalar=pg_sb[:, ec:ec + 1], in1=rstd, op0=ALU.mult, op1=ALU.mult),
                    r=["outT", "rstd", "pg"], w=["tq%d" % xs])
                P.op("pool", lambda gp, xs=xs: gp.tensor_tensor(xt[xs], tq[xs], xt[xs], ALU.add),
                     r=["tq%d" % xs, "xt%d" % xs], w=["xt%d" % xs])
                P.dma("sp", XO[ec * 128:(ec + 1) * 128, tsl], xt[xs], "xts%d" % xs, r=["xt%d" % xs])
        P.finish()
    return nc


def _rope_tables(j):
    t = np.arange(T) + T * j
    row = (t // 64).astype(np.float32)
    col = (t % 64).astype(np.float32)
    freqs = (np.float32(10000.0) ** (-np.arange(32, dtype=np.float32) / np.float32(32))).astype(np.float32)
    cs = np.zeros((128, 2, T), np.float32)
    for e in range(128):
        pos = row if e < 64 else col
        ang = (pos * freqs[e % 32]).astype(np.float32)
        cs[e, 0] = np.cos(ang)
        cs[e, 1] = np.sin(ang)
    return cs


def _partner():
    e = np.arange(128)
    first = (e % 64) < 32
    return np.where(first, e + 32, e - 32), np.where(first, -1.0, 1.0)


def _rm():
    p, sgn = _partner()
    m = np.zeros((128, 128), np.float32)
    m[p, np.arange(128)] = sgn
    return m


def _valid_table(j):
    v = np.zeros((4, 8, 2, 64, 8, 64), np.float32)
    for g in range(4):
        for kt in range(8):
            for i in range(2):
                krow = 32 * j + 8 * g - 4 + 2 * kt + i
                for jq in range(8):
                    r = 32 * j + 8 * g + jq
                    st = min(max(r - 4, 0), 120)
                    if st <= krow < st + 8:
                        v[g, kt, i, :, jq, :] = 1.0
    return v.reshape(32, 128, 512).astype(NPBF)


def _t3_table(rpb_l):
    kc = np.arange(64)[:, None]
    qc = np.arange(64)[None, :]
    cs = np.clip(qc - 8, 0, 48)
    colok = (kc >= cs) & (kc < cs + 16)
    dci = np.clip(kc - qc + 15, 0, 30)
    out = np.full((8, 64, 23, 64), NEG, np.float32)
    for m in range(23):
        dr = 11 - m
        if abs(dr) > 7:
            continue
        g = rpb_l[:, dr + 7, :][:, dci]
        out[:, :, m, :] = np.where(colok[None], g, np.float32(NEG))
    return out


def _bc_table():
    p = np.arange(128)[:, None]
    f = np.arange(256)[None, :]
    dist = np.abs(f - p - 64)
    out = np.zeros((12, 128, 256), np.float32)
    for head in range(12):
        d = DILS[head // 4]
        slope = np.float32(2.0) ** np.float32(-8.0 * (head + 1) / 12)
        out[head] = np.where(dist <= 64, -slope * np.float32(d) * dist.astype(np.float32), np.float32(NEG))
    return out


_NC = {}


def _get(name):
    if name not in _NC:
        _NC[name] = build_p1() if name == "p1" else build_p2()
    return _NC[name]


def _col16(v):
    return np.ascontiguousarray(v.reshape(-1, 128).T)


def run_p1(xTs, l, inputs):
    nc = _get("p1")
    w_in = np.ascontiguousarray(inputs["w_in"][l])
    part, _ = _partner()
    cst = np.zeros((128, 68), np.float32)
    cst[:, 0:16] = _col16(inputs["pre_norm_g"][l])
    cst[:, 16:64] = _col16(inputs["b_gate"][l])
    qg, kg = inputs["q_norm_g"][l], inputs["k_norm_g"][l]
    cst[:, 64], cst[:, 65], cst[:, 66], cst[:, 67] = qg, qg[part], kg, kg[part]
    rm = _rm()
    maps = []
    for c in range(8):
        maps.append({"xT": xTs[c], "w_in": w_in, "cst": cst, "cossin": _rope_tables(c % 4), "rm": rm})
    res = run_bass_kernel_spmd(nc, maps, core_ids=list(range(8)))
    return [(np.asarray(r["FT"]), np.asarray(r["VT"])) for r in res.results]


def _win(full, lo, n, axis):
    L = full.shape[axis]
    shp = list(full.shape)
    shp[axis] = n
    out = np.zeros(shp, full.dtype)
    a, b = max(lo, 0), min(lo + n, L)
    src = [slice(None)] * full.ndim
    dst = [slice(None)] * full.ndim
    src[axis] = slice(a, b)
    dst[axis] = slice(a - lo, b - lo)
    out[tuple(dst)] = full[tuple(src)]
    return out


def run_p2(xTs, p1, l, inputs):
    nc = _get("p2")
    T3 = _t3_table(np.asarray(inputs["rpb"][l]))
    BC = _bc_table()
    wa = np.ascontiguousarray(inputs["w_branch_a"][l])
    wb = np.ascontiguousarray(inputs["w_branch_b"][l])
    wc = np.ascontiguousarray(inputs["w_branch_c"][l])
    wo = np.ascontiguousarray(inputs["w_out"][l])
    pg = _col16(inputs["post_norm_g"][l])
    maps = []
    for b in range(2):
        FTs = [p1[4 * b + j][0] for j in range(4)]
        VTs = [p1[4 * b + j][1] for j in range(4)]
        Kfull = np.concatenate(FTs, axis=2)
        Vfull = np.concatenate(VTs, axis=0)
        KBf = np.ascontiguousarray(Kfull[F_KB:F_KB + 2])
        VBf = np.ascontiguousarray(Vfull[:, 1024:1280])
        for j in range(4):
            c = 4 * b + j
            m = {"FT": FTs[j], "KB": KBf, "VB": VBf, "T3": T3, "BC": BC, "xT": xTs[c],
                 "wa": wa, "wb": wb, "wc": wc, "wo": wo, "pg": pg, "VALID": _valid_table(j)}
            m["KA"] = _win(Kfull[F_KA:F_KA + 8], T * j - 256, 2560, 2)
            m["VA"] = _win(Vfull[:, 0:1024], T * j - 256, 2560, 0)
            for g, d in enumerate(DILS):
                m["KC%d" % g] = _win(Kfull[F_KC + 4 * g:F_KC + 4 * g + 4], T * j - 64 * d, T + 128 * d, 2)
                m["VC%d" % g] = _win(Vfull[:, 1280 + 512 * g:1280 + 512 * g + 512], T * j - 64 * d, T + 128 * d, 0)
            cb = np.zeros((128, 2), np.float32)
            if j == 0:
                cb[:64, 0] = NEG
            if j == 3:
                cb[64:, 1] = NEG
            m["CB"] = cb
            maps.append(m)
    res = run_bass_kernel_spmd(nc, maps, core_ids=list(range(8)))
    if DEBUG_UT:
        return [(np.asarray(r["XO"]), np.asarray(r["UT"])) for r in res.results]
    return [np.asarray(r["XO"]) for r in res.results]


def kernel(x, pre_norm_g, w_in, b_gate, q_norm_g, k_norm_g, rpb, w_branch_a, w_branch_b, w_branch_c,
           w_out, post_norm_g):
    inputs = dict(pre_norm_g=np.asarray(pre_norm_g), w_in=np.asarray(w_in), b_gate=np.asarray(b_gate),
                  q_norm_g=np.asarray(q_norm_g), k_norm_g=np.asarray(k_norm_g), rpb=np.asarray(rpb),
                  w_branch_a=np.asarray(w_branch_a), w_branch_b=np.asarray(w_branch_b),
                  w_branch_c=np.asarray(w_branch_c), w_out=np.asarray(w_out), post_norm_g=np.asarray(post_norm_g))
    x = np.asarray(x)
    xTs = [np.ascontiguousarray(x[c // 4, (c % 4) * T:(c % 4 + 1) * T, :].T) for c in range(8)]
    for l in range(4):
        p1 = run_p1(xTs, l, inputs)
        xTs = run_p2(xTs, p1, l, inputs)
    out = np.zeros((2, SEQ, D), np.float32)
    for c in range(8):
        out[c // 4, (c % 4) * T:(c % 4 + 1) * T, :] = xTs[c].T
    return out
```

```python
import numpy as np
import ml_dtypes
from contextlib import ExitStack
import concourse.bass as bass
import concourse.mybir as mybir
from concourse.bass_utils import run_bass_kernel_spmd

F32 = mybir.dt.float32
BF16 = mybir.dt.bfloat16
AF = mybir.ActivationFunctionType
ALU = mybir.AluOpType
NPBF = ml_dtypes.bfloat16

D = 2048
T = 2048
SEQ = 8192
NIN = 17920
EPS = 1e-6
SCALE = 128 ** -0.5
NEG = -30000.0
F_QA, F_KA, F_QB, F_KB, F_QC, F_KC, F_ZA, F_ZB, F_ZC, F_G = 0, 8, 16, 24, 26, 38, 50, 58, 66, 70
NF = 118
NV = 2816
DILS = (1, 4, 16)
DEBUG_UT = False


class Prog:
    def __init__(self, nc):
        self.nc = nc
        self.E = dict(pe=nc.tensor, act=nc.scalar, dve=nc.vector, pool=nc.gpsimd, sp=nc.sync)
        self.csem = {e: nc.alloc_semaphore(name="c_" + e) for e in ("pe", "act", "dve", "pool")}
        self.ccnt = {e: 0 for e in self.csem}
        self.dsem = {}
        self.res = {}
        self.seen = {}
        self.nsem = 0

    def _wait(self, e, toks):
        best = {}
        for t in toks:
            if t is None:
                continue
            s, v = t
            k = id(s)
            if k not in best or best[k][1] < v:
                best[k] = (s, v)
        for k, (s, v) in best.items():
            if e == "pe" and s is self.csem["pe"]:
                continue
            if self.seen.get((e, k), 0) >= v:
                continue
            self.seen[(e, k)] = v
            self.E[e].wait_ge(s, v)

    def _deps(self, r, w):
        toks = []
        for x in r:
            st = self.res.setdefault(x, [None, {}])
            toks.append(st[0])
        for x in w:
            st = self.res.setdefault(x, [None, {}])
            toks.append(st[0])
            toks.extend(st[1].values())
        return toks

    def _commit(self, tok, r, w):
        for x in r:
            rd = self.res[x][1]
            k = id(tok[0])
            if k not in rd or rd[k][1] < tok[1]:
                rd[k] = tok
        for x in w:
            self.res[x] = [tok, {}]

    def op(self, e, fn, r=(), w=()):
        self._wait(e, self._deps(r, w))
        ins = fn(self.E[e])
        self.ccnt[e] += 1
        ins.then_inc(self.csem[e], 1)
        self._commit((self.csem[e], self.ccnt[e]), r, w)

    def dma(self, q, out, in_, key, r=(), w=()):
        if key not in self.dsem:
            self.nsem += 1
            self.dsem[key] = [self.nc.alloc_semaphore(name="d%d" % self.nsem), 0]
        d = self.dsem[key]
        toks = [t for t in self._deps(r, w) if t is not None and t[0] is not d[0]]
        self._wait(q, toks)
        d[1] += 16
        self.E[q].dma_start(out=out, in_=in_).then_inc(d[0], 16)
        self._commit((d[0], d[1]), r, w)

    def barrier(self):
        toks = [(s, c) for s, c in ((self.csem[e], self.ccnt[e]) for e in self.csem) if c > 0]
        toks += [(s, v) for (s, v) in self.dsem.values()]
        for e in self.E:
            self._wait(e, toks)
        self.res = {}

    def finish(self):
        self._wait("sp", [(s, v) for (s, v) in self.dsem.values()])


def _mm_group(P, ps_ap, pairs, r, w, first=True, last=True):
    def fn(pe):
        ins = None
        n = len(pairs)
        for i, (l, rh) in enumerate(pairs):
            ins = pe.matmul(ps_ap, l, rh, start=(first and i == 0), stop=(last and i == n - 1))
        return ins
    P.op("pe", fn, r=r, w=w)


def _chunk_kind(c):
    col = c * 128
    if col < 1024: return ("copy", F_QA + c)
    if col < 2048: return ("copy", F_KA + (c - 8))
    if col < 3072: return ("v", None)
    if col < 4096: return ("ropeq", F_QB + (c - 24))
    if col < 4352: return ("ropek", F_KB + (c - 32))
    if col < 4608: return ("v", None)
    if col < 6144: return ("copy", F_QC + (c - 36))
    if col < 7680: return ("copy", F_KC + (c - 48))
    if col < 9216: return ("v", None)
    if col < 10240: return ("silu", F_ZA + (c - 72))
    if col < 11264: return ("silu", F_ZB + (c - 80))
    if col < 11776: return ("silu", F_ZC + (c - 88))
    return ("sig", F_G + (c - 92))


def _vcol(col):
    if col < 3072: return col - 2048
    if col < 4608: return 1024 + col - 4352
    return 1280 + col - 7680


def build_p1():
    nc = bass.Bass("TRN2", target_bir_lowering=False)
    xT = nc.dram_tensor("xT", [D, T], F32, kind="ExternalInput").ap()
    w_in = nc.dram_tensor("w_in", [D, NIN], F32, kind="ExternalInput").ap()
    cst = nc.dram_tensor("cst", [128, 68], F32, kind="ExternalInput").ap()
    cossin = nc.dram_tensor("cossin", [128, 2, T], F32, kind="ExternalInput").ap()
    rm = nc.dram_tensor("rm", [128, 128], F32, kind="ExternalInput").ap()
    FT = nc.dram_tensor("FT", [NF, 128, T], BF16, kind="ExternalOutput").ap()
    VT = nc.dram_tensor("VT", [T, NV], BF16, kind="ExternalOutput").ap()
    P = Prog(nc)
    with ExitStack() as es:
        def sb(name, shape, dt):
            return es.enter_context(nc.sbuf_tensor(name, shape, dt))
        hT = sb("hT", [128, 16, T], BF16)
        xin = sb("xin", [128, 16, 256], F32)
        wbl = [sb("wbl%d" % i, [128, 16, 256], BF16) for i in range(3)]
        stg = [sb("stg%d" % i, [128, T], BF16) for i in range(2)]
        vst = [sb("vst%d" % i, [128, 16, 256], BF16) for i in range(2)]
        cs_sb = sb("cs_sb", [128, 2, T], F32)
        cst_sb = sb("cst_sb", [128, 68], F32)
        rm_sb = sb("rm_sb", [128, 128], BF16)
        ones = sb("ones", [128, 128], BF16)
        sq = [sb("sq%d" % i, [128, 512], BF16) for i in range(2)]
        qb16 = sb("qb16", [128, 512], BF16)
        sd = sb("sd", [128, 512], F32)
        rstd = sb("rstd", [128, 512], F32)
        t1 = sb("t1", [128, 512], F32)
        t2 = sb("t2", [128, 512], F32)
        t3 = sb("t3", [128, 512], F32)
        ps = es.enter_context(nc.psum_tensor("ps", [128, 8, 512], F32))

        P.dma("sp", cst_sb[:], cst, "cst", w=["cst"])
        P.dma("sp", cs_sb[:], cossin, "cs", w=["cs"])
        P.dma("pool", rm_sb[:], rm, "rm", w=["rm"])
        P.op("pool", lambda g: g.memset(ones[:], 1.0), w=["ones"])

        xv = xT.rearrange("(k p) t -> p k t", p=128)
        for t8 in range(8):
            ts = slice(t8 * 256, (t8 + 1) * 256)
            P.dma("sp", xin[:], xv[:, :, ts], "xin", w=["xin"])
            for kc in range(16):
                s = kc % 2
                P.op("act", lambda a, kc=kc, s=s: a.activation(out=sq[s][:, 0:256], in_=xin[:, kc, :], func=AF.Square),
                     r=["xin"], w=["sq%d" % s])
                P.op("pe", lambda pe, kc=kc, s=s: pe.matmul(ps[:, 7, 0:256], ones[:], sq[s][:, 0:256],
                                                           start=(kc == 0), stop=(kc == 15)),
                     r=["sq%d" % s, "ones"], w=["ps7"])
            P.op("act", lambda a: a.activation(out=sd[:, 0:256], in_=ps[:, 7, 0:256], func=AF.Sqrt,
                                               bias=EPS, scale=1.0 / D), r=["ps7"], w=["sd"])
            P.op("dve", lambda v: v.reciprocal(rstd[:, 0:256], sd[:, 0:256]), r=["sd"], w=["rstd"])
            for kc in range(16):
                P.op("dve", lambda v, kc=kc, ts=ts: v.scalar_tensor_tensor(
                    out=hT[:, kc, ts], in0=xin[:, kc, :], scalar=cst_sb[:, kc:kc + 1], in1=rstd[:, 0:256],
                    op0=ALU.mult, op1=ALU.mult), r=["xin", "rstd", "cst"], w=["hT"])

        wv = w_in.rearrange("(k p) n -> p k n", p=128)
        NB = NIN // 256

        P.barrier()
        wst1 = sb("wst1", [128, 16, 256], F32)
        wst = [xin, wst1]

        def load_w(b):
            s2, s3 = b % 2, b % 3
            P.dma("sp", wst[s2][:], wv[:, :, b * 256:(b + 1) * 256], "wst%d" % s2, w=["wst%d" % s2])
            P.op("dve", lambda v: v.tensor_copy(wbl[s3][:, 0:8, :], wst[s2][:, 0:8, :]),
                 r=["wst%d" % s2], w=["w%da" % s3])
            P.op("pool", lambda g: g.tensor_copy(wbl[s3][:, 8:16, :], wst[s2][:, 8:16, :]),
                 r=["wst%d" % s2], w=["w%db" % s3])

        load_w(0)
        load_w(1)
        bank = [0]
        cp = [0]
        vti = [0]

        def nbank():
            b = bank[0]
            bank[0] = (b + 1) % 6
            return b

        def evac_copy(out_ap, in_ap, r, w):
            cp[0] ^= 1
            if cp[0]:
                P.op("act", lambda a: a.activation(out=out_ap, in_=in_ap, func=AF.Copy), r=r, w=w)
            else:
                P.op("dve", lambda v: v.tensor_copy(out_ap, in_ap), r=r, w=w)

        for b in range(NB):
            if b + 2 < NB:
                load_w(b + 2)
            ws = b % 3
            wt = wbl[ws]
            kind0 = _chunk_kind(2 * b)[0]
            if kind0 == "v":
                vs = vti[0] % 2
                vti[0] += 1
                for s16 in range(16):
                    bk = nbank()
                    _mm_group(P, ps[:, bk, 0:256],
                              [(hT[:, kc, s16 * 128:(s16 + 1) * 128], wt[:, kc, :]) for kc in range(16)],
                              r=["w%da" % ws, "w%db" % ws, "hT"], w=["ps%d" % bk])
                    evac_copy(vst[vs][:, s16, :], ps[:, bk, 0:256], ["ps%d" % bk], ["vst%d" % vs])
                vc0 = _vcol(b * 256)
                P.dma("sp", VT.rearrange("(s p) c -> p s c", p=128)[:, :, vc0:vc0 + 256], vst[vs][:],
                      "vst%d" % vs, r=["vst%d" % vs])
                continue
            for ci in range(2):
                c = 2 * b + ci
                kind, fi = _chunk_kind(c)
                ss = c % 2
                for tt in range(4):
                    tsl = slice(tt * 512, (tt + 1) * 512)
                    bk = nbank()
                    _mm_group(P, ps[:, bk, :],
                              [(wt[:, kc, ci * 128:(ci + 1) * 128], hT[:, kc, tsl]) for kc in range(16)],
                              r=["w%da" % ws, "w%db" % ws, "hT"], w=["ps%d" % bk])
                    pr = ["ps%d" % bk]
                    so = stg[ss][:, tsl]
                    sw = ["stg%d" % ss]
                    if kind == "copy":
                        evac_copy(so, ps[:, bk, :], pr, sw)
                    elif kind == "silu":
                        P.op("act", lambda a, so=so, bk=bk: a.activation(out=so, in_=ps[:, bk, :], func=AF.Silu),
                             r=pr, w=sw)
                    elif kind == "sig":
                        gi = 16 + (fi - F_G)
                        P.op("act", lambda a, so=so, bk=bk, gi=gi: a.activation(
                            out=so, in_=ps[:, bk, :], func=AF.Sigmoid, bias=cst_sb[:, gi:gi + 1]),
                            r=pr + ["cst"], w=sw)
                    else:
                        gcol = 64 if kind == "ropeq" else 66
                        P.op("act", lambda a, bk=bk: a.activation(out=qb16[:], in_=ps[:, bk, :], func=AF.Copy),
                             r=pr, w=["qb16"])
                        P.op("act", lambda a, bk=bk: a.activation(out=sq[0][:], in_=ps[:, bk, :], func=AF.Square),
                             r=pr, w=["sq0"])
                        P.op("pe", lambda pe: pe.matmul(ps[:, 6, :], ones[:], sq[0][:], start=True, stop=True),
                             r=["sq0", "ones"], w=["ps6"])
                        P.op("pe", lambda pe: pe.matmul(ps[:, 7, :], rm_sb[:], qb16[:], start=True, stop=True),
                             r=["qb16", "rm"], w=["ps7"])
                        P.op("act", lambda a: a.activation(out=sd[:], in_=ps[:, 6, :], func=AF.Sqrt,
                                                           bias=EPS, scale=1.0 / 128), r=["ps6"], w=["sd"])
                        P.op("dve", lambda v: v.reciprocal(rstd[:], sd[:]), r=["sd"], w=["rstd"])
                        P.op("dve", lambda v, bk=bk, tsl=tsl, gcol=gcol: v.scalar_tensor_tensor(
                            out=t1[:], in0=ps[:, bk, :], scalar=cst_sb[:, gcol:gcol + 1], in1=cs_sb[:, 0, tsl],
                            op0=ALU.mult, op1=ALU.mult), r=pr + ["cst", "cs"], w=["t1"])
                        P.op("dve", lambda v, tsl=tsl, gcol=gcol: v.scalar_tensor_tensor(
                            out=t2[:], in0=ps[:, 7, :], scalar=cst_sb[:, gcol + 1:gcol + 2], in1=cs_sb[:, 1, tsl],
                            op0=ALU.mult, op1=ALU.mult), r=["ps7", "cst", "cs"], w=["t2"])
                        P.op("dve", lambda g: g.tensor_tensor(t3[:], t1[:], t2[:], ALU.add),
                             r=["t1", "t2"], w=["t3"])
                        P.op("dve", lambda v, so=so: v.tensor_tensor(so, t3[:], rstd[:], ALU.mult),
                             r=["t3", "rstd"], w=sw)
                P.dma("sp", FT[fi], stg[ss][:], "stg%d" % ss, r=["stg%d" % ss])
        P.finish()
    return nc


class Arena:
    def __init__(self, t, n):
        self.t, self.n, self.o = t, n, 0

    def reset(self):
        self.o = 0

    def alloc(self, *shape):
        sz = int(np.prod(shape))
        sz_al = (sz + 31) // 32 * 32
        assert self.o + sz_al <= self.n, (self.o, sz_al, self.n)
        ap = self.t[:, self.o:self.o + sz]
        self.o += sz_al
        if len(shape) == 2:
            return ap.rearrange("p (a b) -> p a b", b=shape[1])
        return ap


def build_p2():
    nc = bass.Bass("TRN2", target_bir_lowering=False)
    dt = lambda name, shape, d, kind="ExternalInput": nc.dram_tensor(name, shape, d, kind=kind).ap()
    FT = dt("FT", [NF, 128, T], BF16)
    KB = dt("KB", [2, 128, SEQ], BF16)
    VB = dt("VB", [SEQ, 256], BF16)
    KA = dt("KA", [8, 128, 2560], BF16)
    VA = dt("VA", [2560, 1024], BF16)
    KC = [dt("KC%d" % g, [4, 128, T + 128 * d], BF16) for g, d in enumerate(DILS)]
    VC = [dt("VC%d" % g, [T + 128 * d, 512], BF16) for g, d in enumerate(DILS)]
    T3 = dt("T3", [8, 64, 23, 64], F32)
    VALID = dt("VALID", [32, 128, 512], BF16)
    CB = dt("CB", [128, 2], F32)
    BC = dt("BC", [12, 128, 256], F32)
    xT = dt("xT", [D, T], F32)
    wa = dt("wa", [1024, D], F32)
    wb = dt("wb", [1024, D], F32)
    wc = dt("wc", [512, D], F32)
    wo = dt("wo", [D, D], F32)
    pg = dt("pg", [128, 16], F32)
    XO = dt("XO", [D, T], F32, kind="ExternalOutput")
    UT = nc.dram_tensor("UT", [20, 128, T], BF16, kind=("ExternalOutput" if DEBUG_UT else "Internal")).ap()
    P = Prog(nc)
    with ExitStack() as es:
        NBF, NFP = 52 * 1024, 13 * 1024
        abf_t = es.enter_context(nc.sbuf_tensor("abf", [128, NBF], BF16))
        afp_t = es.enter_context(nc.sbuf_tensor("afp", [128, NFP], F32))
        ones = es.enter_context(nc.sbuf_tensor("ones", [128, 128], BF16))
        zeros = es.enter_context(nc.sbuf_tensor("zeros", [128, 128], BF16))
        cb_sb = es.enter_context(nc.sbuf_tensor("cb_sb", [128, 2], F32))
        pg_sb = es.enter_context(nc.sbuf_tensor("pg_sb", [128, 16], F32))
        ps = es.enter_context(nc.psum_tensor("ps", [128, 8, 512], F32))
        BFA = Arena(abf_t, NBF)
        FPA = Arena(afp_t, NFP)
        P.op("pool", lambda g: g.memset(ones[:], 1.0), w=["ones"])
        P.op("pool", lambda g: g.memset(zeros[:], 0.0), w=["zeros"])
        P.dma("sp", cb_sb[:], CB, "cb", w=["cb"])
        P.dma("sp", pg_sb[:], pg, "pg", w=["pg"])

        def finalize(qs, nb, db, rden, y, zt, ub, dst):
            P.op("dve", lambda v: v.reciprocal(rden[qs], ps[:, db, :]), r=["ps%d" % db], w=["rden%d" % qs])
            P.op("dve", lambda v: v.tensor_tensor(y[qs], ps[:, nb, :], rden[qs], ALU.mult),
                 r=["ps%d" % nb, "rden%d" % qs], w=["y%d" % qs])
            P.op("dve", lambda g: g.tensor_tensor(ub[qs], y[qs], zt[qs], ALU.mult),
                 r=["y%d" % qs, "z%d" % qs], w=["ub%d" % qs])
            P.dma("sp", dst, ub[qs], "ub%d" % qs, r=["ub%d" % qs])

        kbT = BFA.alloc(2, SEQ)
        vb = BFA.alloc(64, 256)
        qT = [BFA.alloc(512) for _ in range(2)]
        zt = [BFA.alloc(512) for _ in range(2)]
        ub = [BFA.alloc(512) for _ in range(2)]
        pT = [BFA.alloc(512) for _ in range(3)]
        rden = [FPA.alloc(512) for _ in range(2)]
        y = [FPA.alloc(512) for _ in range(2)]
        accA = [FPA.alloc(512) for _ in range(2)]
        accB = [FPA.alloc(512) for _ in range(2)]
        accbf = [BFA.alloc(512) for _ in range(2)]
        for kv in range(2):
            P.dma("sp", kbT[:, kv, :], KB[kv], "kbT", w=["kbT"])
        P.dma("sp", vb, VB.rearrange("(k p) c -> p k c", p=128), "vb", w=["vb"])
        it = 0
        for h in range(8):
            kv = h // 4
            for qt in range(4):
                qs = it % 2
                it += 1
                tsl = slice(qt * 512, (qt + 1) * 512)
                P.dma("sp", qT[qs], FT[F_QB + h, :, tsl], "q%d" % qs, w=["q%d" % qs])
                P.dma("sp", zt[qs], FT[F_ZB + h, :, tsl], "z%d" % qs, w=["z%d" % qs])
                nb, db = 3 + 2 * qs, 4 + 2 * qs

                def qk(kt, qs=qs, kv=kv):
                    s = kt % 3
                    P.op("pe", lambda pe: pe.matmul(ps[:, s, :], kbT[:, kv, kt * 128:(kt + 1) * 128], qT[qs],
                                                    start=True, stop=True),
                         r=["kbT", "q%d" % qs], w=["ps%d" % s])
                qk(0)
                qk(1)
                for kt in range(64):
                    s = kt % 3
                    if kt + 2 < 64:
                        qk(kt + 2)
                    P.op("act", lambda a, s=s: a.activation(out=pT[s], in_=ps[:, s, :], func=AF.Exp, scale=SCALE),
                         r=["ps%d" % s], w=["p%d" % s])
                    P.op("pe", lambda pe, s=s, kt=kt, kv=kv, nb=nb: pe.matmul(
                        ps[:, nb, :], vb[:, kt, kv * 128:(kv + 1) * 128], pT[s], start=(kt == 0), stop=(kt == 63)),
                        r=["vb", "p%d" % s], w=["ps%d" % nb])
                    acc_ = accA if kt % 2 == 0 else accB
                    an = ("accA%d" if kt % 2 == 0 else "accB%d") % qs
                    if kt < 2:
                        P.op("dve", lambda v, s=s, qs=qs, acc_=acc_: v.tensor_copy(acc_[qs], pT[s]), r=["p%d" % s], w=[an])
                    else:
                        P.op("dve", lambda v, s=s, qs=qs, acc_=acc_: v.tensor_tensor(acc_[qs], acc_[qs], pT[s], ALU.add),
                             r=["p%d" % s, an], w=[an])
                P.op("dve", lambda v, qs=qs: v.tensor_tensor(accbf[qs], accA[qs], accB[qs], ALU.add),
                     r=["accA%d" % qs, "accB%d" % qs], w=["accbf%d" % qs])
                P.op("pe", lambda pe, qs=qs, db=db: pe.matmul(ps[:, db, :], ones[:], accbf[qs], start=True, stop=True),
                     r=["ones", "accbf%d" % qs], w=["ps%d" % db])
                finalize(qs, nb, db, rden, y, zt, ub, UT[8 + h, :, tsl])
        P.barrier()

        BFA.reset()
        FPA.reset()
        kaT = BFA.alloc(2560)
        va = BFA.alloc(20, 128)
        ebias = BFA.alloc(8, 512)
        valid = BFA.alloc(32, 512)
        msk = [BFA.alloc(512) for _ in range(3)]
        p2 = [BFA.alloc(512) for _ in range(3)]
        qT = [BFA.alloc(512) for _ in range(2)]
        zt = [BFA.alloc(512) for _ in range(2)]
        ub = [BFA.alloc(512) for _ in range(2)]
        pT = [BFA.alloc(512) for _ in range(3)]
        rden = [FPA.alloc(512) for _ in range(2)]
        y = [FPA.alloc(512) for _ in range(2)]
        t3s = [FPA.alloc(512) for _ in range(2)]
        P.dma("sp", valid, VALID.rearrange("n p q -> p n q"), "valid", w=["valid"])
        it = 0
        e_i = 0
        for h in range(8):
            P.dma("sp", kaT, KA[h], "kaT", w=["kaT"])
            P.dma("sp", va, VA.rearrange("(k p) c -> p k c", p=128)[:, :, h * 128:(h + 1) * 128], "va", w=["va"])
            for kt in range(8):
                es_ = e_i % 2
                e_i += 1
                for i in range(2):
                    m0 = 15 - 2 * kt - i
                    P.dma("sp", t3s[es_][i * 64:(i + 1) * 64, :],
                          T3[h, :, m0:m0 + 8, :].rearrange("k m q -> k (m q)"), "t3s%d" % es_, w=["t3s%d" % es_])
                P.op("act", lambda a, kt=kt, es_=es_: a.activation(out=ebias[:, kt, :], in_=t3s[es_], func=AF.Exp),
                     r=["t3s%d" % es_], w=["ebias"])
            for g in range(4):
                qs = it % 2
                it += 1
                tsl = slice(g * 512, (g + 1) * 512)
                P.dma("sp", qT[qs], FT[F_QA + h, :, tsl], "q%d" % qs, w=["q%d" % qs])
                P.dma("sp", zt[qs], FT[F_ZA + h, :, tsl], "z%d" % qs, w=["z%d" % qs])
                nb, db = 3 + 2 * qs, 4 + 2 * qs

                def qk(kt, qs=qs, g=g):
                    s = kt % 3
                    ktile = 4 * g + kt
                    P.op("pe", lambda pe: pe.matmul(ps[:, s, :], kaT[:, ktile * 128:(ktile + 1) * 128], qT[qs],
                                                    start=True, stop=True),
                         r=["kaT", "q%d" % qs], w=["ps%d" % s])
                qk(0)
                qk(1)
                for kt in range(8):
                    s = kt % 3
                    ktile = 4 * g + kt
                    if kt + 2 < 8:
                        qk(kt + 2)
                    P.op("dve", lambda gp, s=s, kt=kt, g=g: gp.tensor_tensor(
                        msk[s], ebias[:, kt, :], valid[:, g * 8 + kt, :], ALU.mult),
                        r=["ebias", "valid"], w=["msk%d" % s])
                    P.op("act", lambda a, s=s: a.activation(out=pT[s], in_=ps[:, s, :], func=AF.Exp, scale=SCALE),
                         r=["ps%d" % s], w=["p%d" % s])
                    P.op("dve", lambda v, s=s: v.tensor_tensor(p2[s], pT[s], msk[s], ALU.mult),
                         r=["p%d" % s, "msk%d" % s], w=["pp%d" % s])
                    P.op("pe", lambda pe, s=s, kt=kt, ktile=ktile, nb=nb: pe.matmul(
                        ps[:, nb, :], va[:, ktile, :], p2[s], start=(kt == 0), stop=(kt == 7)),
                        r=["va", "pp%d" % s], w=["ps%d" % nb])
                    P.op("pe", lambda pe, s=s, kt=kt, db=db: pe.matmul(
                        ps[:, db, :], ones[:], p2[s], start=(kt == 0), stop=(kt == 7)),
                        r=["ones", "pp%d" % s], w=["ps%d" % db])
                finalize(qs, nb, db, rden, y, zt, ub, UT[h, :, tsl])
        P.barrier()

        BFA.reset()
        FPA.reset()
        kcT = [BFA.alloc(T + 128 * 16) for _ in range(2)]
        qcT = [BFA.alloc(T) for _ in range(2)]
        vct = [BFA.alloc(17, 128) for _ in range(2)]
        mC = [BFA.alloc(256) for _ in range(2)]
        pT = [BFA.alloc(512) for _ in range(3)]
        p2 = [BFA.alloc(512) for _ in range(3)]
        ztc = BFA.alloc(T)
        ubc = BFA.alloc(T)
        accn = FPA.alloc(T)
        accd = FPA.alloc(T)
        mst = [FPA.alloc(256) for _ in range(2)]
        ytmp = FPA.alloc(512)
        hi_ = 0
        vi_ = 0
        ui_ = 0
        si_ = 0
        for hh in range(4):
            for gi, d in enumerate(DILS):
                head = 4 * gi + hh
                ks = hi_ % 2
                hi_ += 1
                W = T + 128 * d
                Lq = T // d
                nkt = Lq // 128 + 1
                P.dma("sp", kcT[ks][:, 0:W], KC[gi][hh], "kc%d" % ks, w=["kc%d" % ks])
                P.dma("sp", qcT[ks], FT[F_QC + head], "qc%d" % ks, w=["qc%d" % ks])
                P.dma("sp", mst[ks], BC[head], "mst%d" % ks, w=["mst%d" % ks])
                P.op("act", lambda a, ks=ks: a.activation(out=mC[ks], in_=mst[ks], func=AF.Exp),
                     r=["mst%d" % ks], w=["mC%d" % ks])
                vsrc = VC[gi].rearrange("(k p dd) c -> dd p k c", p=128, dd=d)
                for c in range(d):
                    vs = vi_ % 2
                    vi_ += 1
                    nload = min(nkt, 17)
                    P.dma("sp", vct[vs][:, 0:nload, :], vsrc[c, :, :, hh * 128:(hh + 1) * 128], "vct%d" % vs,
                          w=["vct%d" % vs])
                    for qq in range(max(1, Lq // 512)):
                        qlo, qhi = qq * 512, min(Lq, qq * 512 + 512)
                        NQ = qhi - qlo
                        us = ui_ % 2
                        ui_ += 1
                        nb, db = 3 + 2 * us, 4 + 2 * us
                        kts = [kt for kt in range(nkt) if 128 * kt + 128 > qlo and 128 * kt - 128 < qhi]
                        P.op("pe", lambda pe, nb=nb, NQ=NQ, ks=ks: pe.matmul(
                            ps[:, nb, 0:NQ], zeros[:], qcT[ks][:, 0:NQ], start=True, stop=False,
                            skip_group_check=True), r=["zeros", "qc%d" % ks], w=["ps%d" % nb])
                        P.op("pe", lambda pe, db=db, NQ=NQ, ks=ks: pe.matmul(
                            ps[:, db, 0:NQ], zeros[:], qcT[ks][:, 0:NQ], start=True, stop=False,
                            skip_group_check=True), r=["zeros", "qc%d" % ks], w=["ps%d" % db])
                        for j, kt in enumerate(kts):
                            s = si_ % 3
                            si_ += 1
                            lo = max(qlo, 128 * kt - 128)
                            hi = min(qhi, 128 * kt + 128)
                            N = hi - lo
                            f0 = lo - (128 * kt - 128)
                            k0 = 128 * kt * d + c
                            kap = kcT[ks][:, k0:k0 + 127 * d + 1:d] if d > 1 else kcT[ks][:, k0:k0 + 128]
                            q0 = lo * d + c
                            qap = qcT[ks][:, q0:q0 + (N - 1) * d + 1:d] if d > 1 else qcT[ks][:, q0:q0 + N]
                            P.op("pe", lambda pe, s=s, kap=kap, qap=qap, N=N: pe.matmul(
                                ps[:, s, 0:N], kap, qap, start=True, stop=True),
                                r=["kc%d" % ks, "qc%d" % ks], w=["ps%d" % s])
                            if kt == 0:
                                P.op("act", lambda a, s=s, N=N: a.activation(
                                    out=pT[s][:, 0:N], in_=ps[:, s, 0:N], func=AF.Exp, scale=SCALE,
                                    bias=cb_sb[:, 0:1]), r=["ps%d" % s, "cb"], w=["p%d" % s])
                            elif kt == nkt - 1:
                                P.op("act", lambda a, s=s, N=N: a.activation(
                                    out=pT[s][:, 0:N], in_=ps[:, s, 0:N], func=AF.Exp, scale=SCALE,
                                    bias=cb_sb[:, 1:2]), r=["ps%d" % s, "cb"], w=["p%d" % s])
                            else:
                                P.op("act", lambda a, s=s, N=N: a.activation(
                                    out=pT[s][:, 0:N], in_=ps[:, s, 0:N], func=AF.Exp, scale=SCALE),
                                    r=["ps%d" % s], w=["p%d" % s])
                            P.op("dve", lambda v, s=s, N=N, f0=f0, ks=ks: v.tensor_tensor(
                                p2[s][:, 0:N], pT[s][:, 0:N], mC[ks][:, f0:f0 + N], ALU.mult),
                                r=["p%d" % s, "mC%d" % ks], w=["pp%d" % s])
                            last = (j == len(kts) - 1)
                            o0 = lo - qlo
                            P.op("pe", lambda pe, s=s, N=N, kt=kt, vs=vs, nb=nb, o0=o0, last=last: pe.matmul(
                                ps[:, nb, o0:o0 + N], vct[vs][:, kt, :], p2[s][:, 0:N], start=False, stop=last,
                                skip_group_check=True), r=["vct%d" % vs, "pp%d" % s], w=["ps%d" % nb])
                            P.op("pe", lambda pe, s=s, N=N, db=db, o0=o0, last=last: pe.matmul(
                                ps[:, db, o0:o0 + N], ones[:], p2[s][:, 0:N], start=False, stop=last,
                                skip_group_check=True), r=["ones", "pp%d" % s], w=["ps%d" % db])
                        t0 = qlo * d + c
                        an = accn[:, t0:t0 + (NQ - 1) * d + 1:d] if d > 1 else accn[:, t0:t0 + NQ]
                        ad = accd[:, t0:t0 + (NQ - 1) * d + 1:d] if d > 1 else accd[:, t0:t0 + NQ]
                        if gi == 0:
                            P.op("dve", lambda v, an=an, nb=nb, NQ=NQ: v.tensor_copy(an, ps[:, nb, 0:NQ]),
                                 r=["ps%d" % nb], w=["accn"])
                            P.op("dve", lambda v, ad=ad, db=db, NQ=NQ: v.tensor_copy(ad, ps[:, db, 0:NQ]),
                                 r=["ps%d" % db], w=["accd"])
                        else:
                            P.op("dve", lambda v, an=an, nb=nb, NQ=NQ: v.tensor_tensor(an, ps[:, nb, 0:NQ], an, ALU.add),
                                 r=["ps%d" % nb, "accn"], w=["accn"])
                            P.op("dve", lambda v, ad=ad, db=db, NQ=NQ: v.tensor_tensor(ad, ps[:, db, 0:NQ], ad, ALU.add),
                                 r=["ps%d" % db, "accd"], w=["accd"])
            P.dma("sp", ztc, FT[F_ZC + hh], "ztc", w=["ztc"])
            for qq in range(4):
                tsl = slice(qq * 512, (qq + 1) * 512)
                P.op("dve", lambda v, tsl=tsl: v.reciprocal(ytmp, accd[:, tsl]), r=["accd"], w=["ytmp"])
                P.op("dve", lambda v, tsl=tsl: v.tensor_tensor(ytmp, accn[:, tsl], ytmp, ALU.mult),
                     r=["accn", "ytmp"], w=["ytmp"])
                P.op("dve", lambda gp, tsl=tsl: gp.tensor_tensor(ubc[:, tsl], ytmp, ztc[:, tsl], ALU.mult),
                     r=["ytmp", "ztc"], w=["ubc"])
            P.dma("sp", UT[16 + hh], ubc, "ubc", r=["ubc"])
        P.barrier()

        BFA.reset()
        FPA.reset()
        ut = [BFA.alloc(20, 512) for _ in range(2)]
        g3 = [BFA.alloc(3, 512) for _ in range(2)]
        wbr = [BFA.alloc(20, 256) for _ in range(2)]
        wob = [BFA.alloc(16, 256) for _ in range(2)]
        mT = BFA.alloc(16, 512)
        sqb = [BFA.alloc(512) for _ in range(2)]
        outT = FPA.alloc(16, 512)
        m1 = [FPA.alloc(512) for _ in range(2)]
        m2 = [FPA.alloc(512) for _ in range(2)]
        xt = [FPA.alloc(512) for _ in range(2)]
        sd = FPA.alloc(512)
        rstd = FPA.alloc(512)
        tq = [FPA.alloc(512) for _ in range(2)]
        wav = wa.rearrange("(k p) n -> p k n", p=128)
        wbv = wb.rearrange("(k p) n -> p k n", p=128)
        wcv = wc.rearrange("(k p) n -> p k n", p=128)
        wov = wo.rearrange("(k p) n -> p k n", p=128)
        gv = FT[F_G:F_G + 48].rearrange("(b c) p t -> c p b t", b=3)
        utv = UT.rearrange("c p t -> p c t")
        gi_ = 0
        xi_ = 0
        bi_ = 0
        oi_ = 0
        bk_ = 0
        for tt in range(4):
            tsl = slice(tt * 512, (tt + 1) * 512)
            us = tt % 2
            P.dma("sp", ut[us], utv[:, :, tsl], "ut%d" % us, w=["ut%d" % us])
            for cb in range(8):
                ws = bi_ % 2
                bi_ += 1
                csl = slice(cb * 256, (cb + 1) * 256)
                P.dma("pool", wbr[ws][:, 0:8, :], wav[:, :, csl], "wbr%d" % ws, w=["wbr%d" % ws])
                P.dma("pool", wbr[ws][:, 8:16, :], wbv[:, :, csl], "wbr%d" % ws, w=["wbr%d" % ws])
                P.dma("pool", wbr[ws][:, 16:20, :], wcv[:, :, csl], "wbr%d" % ws, w=["wbr%d" % ws])
                for ci in range(2):
                    dc = 2 * cb + ci
                    gs = gi_ % 2
                    gi_ += 1
                    P.dma("sp", g3[gs], gv[dc][:, :, tsl], "g3%d" % gs, w=["g3%d" % gs])
                    col = slice(ci * 128, (ci + 1) * 128)
                    banks = [(bk_ + i) % 6 for i in range(3)]
                    bk_ = (bk_ + 3) % 6
                    for bi, (k0, k1) in enumerate(((0, 8), (8, 16), (16, 20))):
                        _mm_group(P, ps[:, banks[bi], :],
                                  [(wbr[ws][:, k, col], ut[us][:, k, :]) for k in range(k0, k1)],
                                  r=["wbr%d" % ws, "ut%d" % us], w=["ps%d" % banks[bi]])
                    P.op("dve", lambda v, gs=gs, b0=banks[0]: v.tensor_tensor(m1[gs], ps[:, b0, :], g3[gs][:, 0, :], ALU.mult),
                         r=["ps%d" % banks[0], "g3%d" % gs], w=["m1%d" % gs])
                    P.op("dve", lambda v, gs=gs, b1=banks[1]: v.tensor_tensor(m2[gs], ps[:, b1, :], g3[gs][:, 1, :], ALU.mult),
                         r=["ps%d" % banks[1], "g3%d" % gs], w=["m2%d" % gs])
                    P.op("dve", lambda gp, gs=gs: gp.tensor_tensor(m1[gs], m1[gs], m2[gs], ALU.add),
                         r=["m1%d" % gs, "m2%d" % gs], w=["m1%d" % gs])
                    P.op("dve", lambda v, gs=gs, b2=banks[2]: v.tensor_tensor(m2[gs], ps[:, b2, :], g3[gs][:, 2, :], ALU.mult),
                         r=["ps%d" % banks[2], "g3%d" % gs, "m1%d" % gs], w=["m2%d" % gs])
                    P.op("dve", lambda gp, gs=gs, dc=dc: gp.tensor_tensor(mT[:, dc, :], m1[gs], m2[gs], ALU.add),
                         r=["m1%d" % gs, "m2%d" % gs], w=["mT"])
            for eb in range(8):
                ws = oi_ % 2
                oi_ += 1
                csl = slice(eb * 256, (eb + 1) * 256)
                P.dma("pool", wob[ws], wov[:, :, csl], "wob%d" % ws, w=["wob%d" % ws])
                for ci in range(2):
                    ec = 2 * eb + ci
                    col = slice(ci * 128, (ci + 1) * 128)
                    bk = bk_
                    bk_ = (bk_ + 1) % 6
                    _mm_group(P, ps[:, bk, :], [(wob[ws][:, k, col], mT[:, k, :]) for k in range(16)],
                              r=["wob%d" % ws, "mT"], w=["ps%d" % bk])
                    P.op("act", lambda a, ec=ec, bk=bk: a.activation(out=outT[:, ec, :], in_=ps[:, bk, :], func=AF.Copy),
                         r=["ps%d" % bk], w=["outT"])
                    s = ec % 2
                    P.op("act", lambda a, s=s, bk=bk: a.activation(out=sqb[s], in_=ps[:, bk, :], func=AF.Square),
                         r=["ps%d" % bk], w=["sqb%d" % s])
                    P.op("pe", lambda pe, s=s, ec=ec: pe.matmul(ps[:, 7, :], ones[:], sqb[s], start=(ec == 0), stop=(ec == 15)),
                         r=["sqb%d" % s, "ones"], w=["ps7"])
            P.op("act", lambda a: a.activation(out=sd, in_=ps[:, 7, :], func=AF.Sqrt, bias=EPS, scale=1.0 / D),
                 r=["ps7"], w=["sd"])
            P.op("dve", lambda v: v.reciprocal(rstd, sd), r=["sd"], w=["rstd"])
            for ec in range(16):
                xs = xi_ % 2
                xi_ += 1
                P.dma("sp", xt[xs], xT[ec * 128:(ec + 1) * 128, tsl], "xt%d" % xs, w=["xt%d" % xs])
                P.op("dve", lambda v, ec=ec, xs=xs: v.scalar_tensor_tensor(
                    out=tq[xs], in0=outT[:, ec, :], scalar=pg_sb[:, ec:ec + 1], in1=rstd, op0=ALU.mult, op1=ALU.mult),
                    r=["outT", "rstd", "pg"], w=["tq%d" % xs])
                P.op("dve", lambda gp, xs=xs: gp.tensor_tensor(xt[xs], tq[xs], xt[xs], ALU.add),
                     r=["tq%d" % xs, "xt%d" % xs], w=["xt%d" % xs])
                P.dma("sp", XO[ec * 128:(ec + 1) * 128, tsl], xt[xs], "xts%d" % xs, r=["xt%d" % xs])
        P.finish()
    return nc


def _rope_tables(j):
    t = np.arange(T) + T * j
    row = (t // 64).astype(np.float32)
    col = (t % 64).astype(np.float32)
    freqs = (np.float32(10000.0) ** (-np.arange(32, dtype=np.float32) / np.float32(32))).astype(np.float32)
    cs = np.zeros((128, 2, T), np.float32)
    for e in range(128):
        pos = row if e < 64 else col
        ang = (pos * freqs[e % 32]).astype(np.float32)
        cs[e, 0] = np.cos(ang)
        cs[e, 1] = np.sin(ang)
    return cs


def _partner():
    e = np.arange(128)
    first = (e % 64) < 32
    return np.where(first, e + 32, e - 32), np.where(first, -1.0, 1.0)


def _rm():
    p, sgn = _partner()
    m = np.zeros((128, 128), np.float32)
    m[p, np.arange(128)] = sgn
    return m


def _valid_table(j):
    v = np.zeros((4, 8, 2, 64, 8, 64), np.float32)
    for g in range(4):
        for kt in range(8):
            for i in range(2):
                krow = 32 * j + 8 * g - 4 + 2 * kt + i
                for jq in range(8):
                    r = 32 * j + 8 * g + jq
                    st = min(max(r - 4, 0), 120)
                    if st <= krow < st + 8:
                        v[g, kt, i, :, jq, :] = 1.0
    return v.reshape(32, 128, 512).astype(NPBF)


def _t3_table(rpb_l):
    kc = np.arange(64)[:, None]
    qc = np.arange(64)[None, :]
    cs = np.clip(qc - 8, 0, 48)
    colok = (kc >= cs) & (kc < cs + 16)
    dci = np.clip(kc - qc + 15, 0, 30)
    out = np.full((8, 64, 23, 64), NEG, np.float32)
    for m in range(23):
        dr = 11 - m
        if abs(dr) > 7:
            continue
        g = rpb_l[:, dr + 7, :][:, dci]
        out[:, :, m, :] = np.where(colok[None], g, np.float32(NEG))
    return out


def _bc_table():
    p = np.arange(128)[:, None]
    f = np.arange(256)[None, :]
    dist = np.abs(f - p - 64)
    out = np.zeros((12, 128, 256), np.float32)
    for head in range(12):
        d = DILS[head // 4]
        slope = np.float32(2.0) ** np.float32(-8.0 * (head + 1) / 12)
        out[head] = np.where(dist <= 64, -slope * np.float32(d) * dist.astype(np.float32), np.float32(NEG))
    return out


_NC = {}


def _get(name):
    if name not in _NC:
        _NC[name] = build_p1() if name == "p1" else build_p2()
    return _NC[name]


def _col16(v):
    return np.ascontiguousarray(v.reshape(-1, 128).T)


def run_p1(xTs, l, inputs):
    nc = _get("p1")
    w_in = np.ascontiguousarray(inputs["w_in"][l])
    part, _ = _partner()
    cst = np.zeros((128, 68), np.float32)
    cst[:, 0:16] = _col16(inputs["pre_norm_g"][l])
    cst[:, 16:64] = _col16(inputs["b_gate"][l])
    qg, kg = inputs["q_norm_g"][l], inputs["k_norm_g"][l]
    cst[:, 64], cst[:, 65], cst[:, 66], cst[:, 67] = qg, qg[part], kg, kg[part]
    rm = _rm()
    maps = []
    for c in range(8):
        maps.append({"xT": xTs[c], "w_in": w_in, "cst": cst, "cossin": _rope_tables(c % 4), "rm": rm})
    res = run_bass_kernel_spmd(nc, maps, core_ids=list(range(8)))
    return [(np.asarray(r["FT"]), np.asarray(r["VT"])) for r in res.results]


def _win(full, lo, n, axis):
    L = full.shape[axis]
    shp = list(full.shape)
    shp[axis] = n
    out = np.zeros(shp, full.dtype)
    a, b = max(lo, 0), min(lo + n, L)
    src = [slice(None)] * full.ndim
    dst = [slice(None)] * full.ndim
    src[axis] = slice(a, b)
    dst[axis] = slice(a - lo, b - lo)
    out[tuple(dst)] = full[tuple(src)]
    return out


def run_p2(xTs, p1, l, inputs):
    nc = _get("p2")
    T3 = _t3_table(np.asarray(inputs["rpb"][l]))
    BC = _bc_table()
    wa = np.ascontiguousarray(inputs["w_branch_a"][l])
    wb = np.ascontiguousarray(inputs["w_branch_b"][l])
    wc = np.ascontiguousarray(inputs["w_branch_c"][l])
    wo = np.ascontiguousarray(inputs["w_out"][l])
    pg = _col16(inputs["post_norm_g"][l])
    maps = []
    for b in range(2):
        FTs = [p1[4 * b + j][0] for j in range(4)]
        VTs = [p1[4 * b + j][1] for j in range(4)]
        Kfull = np.concatenate(FTs, axis=2)
        Vfull = np.concatenate(VTs, axis=0)
        KBf = np.ascontiguousarray(Kfull[F_KB:F_KB + 2])
        VBf = np.ascontiguousarray(Vfull[:, 1024:1280])
        for j in range(4):
            c = 4 * b + j
            m = {"FT": FTs[j], "KB": KBf, "VB": VBf, "T3": T3, "BC": BC, "xT": xTs[c],
                 "wa": wa, "wb": wb, "wc": wc, "wo": wo, "pg": pg, "VALID": _valid_table(j)}
            m["KA"] = _win(Kfull[F_KA:F_KA + 8], T * j - 256, 2560, 2)
            m["VA"] = _win(Vfull[:, 0:1024], T * j - 256, 2560, 0)
            for g, d in enumerate(DILS):
                m["KC%d" % g] = _win(Kfull[F_KC + 4 * g:F_KC + 4 * g + 4], T * j - 64 * d, T + 128 * d, 2)
                m["VC%d" % g] = _win(Vfull[:, 1280 + 512 * g:1280 + 512 * g + 512], T * j - 64 * d, T + 128 * d, 0)
            cb = np.zeros((128, 2), np.float32)
            if j == 0:
                cb[:64, 0] = NEG
            if j == 3:
                cb[64:, 1] = NEG
            m["CB"] = cb
            maps.append(m)
    res = run_bass_kernel_spmd(nc, maps, core_ids=list(range(8)))
    if DEBUG_UT:
        return [(np.asarray(r["XO"]), np.asarray(r["UT"])) for r in res.results]
    return [np.asarray(r["XO"]) for r in res.results]


def kernel(x, pre_norm_g, w_in, b_gate, q_norm_g, k_norm_g, rpb, w_branch_a, w_branch_b, w_branch_c,
           w_out, post_norm_g):
    inputs = dict(pre_norm_g=np.asarray(pre_norm_g), w_in=np.asarray(w_in), b_gate=np.asarray(b_gate),
                  q_norm_g=np.asarray(q_norm_g), k_norm_g=np.asarray(k_norm_g), rpb=np.asarray(rpb),
                  w_branch_a=np.asarray(w_branch_a), w_branch_b=np.asarray(w_branch_b),
                  w_branch_c=np.asarray(w_branch_c), w_out=np.asarray(w_out), post_norm_g=np.asarray(post_norm_g))
    x = np.asarray(x)
    xTs = [np.ascontiguousarray(x[c // 4, (c % 4) * T:(c % 4 + 1) * T, :].T) for c in range(8)]
    for l in range(4):
        p1 = run_p1(xTs, l, inputs)
        xTs = run_p2(xTs, p1, l, inputs)
    out = np.zeros((2, SEQ, D), np.float32)
    for c in range(8):
        out[c // 4, (c % 4) * T:(c % 4 + 1) * T, :] = xTs[c].T
    return out
```
